# Optimizing a Trainium2 kernel written in Bass

```python
import math
import jax
import jax.numpy as jnp
from jax import lax
import numpy as np

D_MODEL = 1024
BATCH = 8
SEQ = 2048
DEPTH = 2
DEC_BATCH = 16
DEC_SEQ = 4096
PAST_LEN = 128

HEAD_DIM = 64
GRID_W = 64
NA_HEADS = 8
NA_WIN_ROWS = 8
NA_WIN_COLS = 16
NA_QCOLS = 16
NA_KCOLS = 32
GQA_Q_HEADS = 8
GQA_KV_HEADS = 2
AXIAL_THETA = 10000.0
DIFF_HEADS = 8
D_FF = 4 * D_MODEL
ROPE_THETA = 10000.0
Q_BLOCK = 128
NORM_EPS = 1e-6
QK_NORM_EPS = 1e-6
SUBLN_EPS = 1e-5

A_WIDTH = NA_HEADS * HEAD_DIM
B_Q_WIDTH = GQA_Q_HEADS * HEAD_DIM
B_KV_WIDTH = GQA_KV_HEADS * HEAD_DIM
EVEN_IN = 3 * A_WIDTH + B_Q_WIDTH + 2 * B_KV_WIDTH
EVEN_OUT = A_WIDTH + B_Q_WIDTH
EVEN_SPLITS = [A_WIDTH, 2 * A_WIDTH, 3 * A_WIDTH, 3 * A_WIDTH + B_Q_WIDTH,
               3 * A_WIDTH + B_Q_WIDTH + B_KV_WIDTH]
DIFF_WIDTH = 2 * DIFF_HEADS * HEAD_DIM
ODD_IN = 3 * DIFF_WIDTH
ODD_SPLITS = [DIFF_WIDTH, 2 * DIFF_WIDTH]
N_EVEN = (DEPTH + 1) // 2
N_ODD = DEPTH // 2

kernel_name = "hybrid_natten_gqa_diffattn_encoder"


def rmsnorm(x, g, eps=NORM_EPS):
    xf = x.astype(jnp.float32)
    y = xf * lax.rsqrt(jnp.mean(xf * xf, axis=-1, keepdims=True) + eps)
    return (y * g.astype(jnp.float32)).astype(x.dtype)


def rope_angles(pos, dim, theta):
    inv_freq = 1.0 / jnp.power(theta, jnp.arange(0, dim, 2, dtype=jnp.float32) / dim)
    ang = pos.astype(jnp.float32)[:, None] * inv_freq[None, :]
    return jnp.cos(ang), jnp.sin(ang)


def apply_rope(x, cos, sin):
    xf = x.astype(jnp.float32)
    half = xf.shape[-1] // 2
    x1, x2 = xf[..., :half], xf[..., half:]
    c = cos[None, :, None, :]
    s = sin[None, :, None, :]
    return jnp.concatenate([x1 * c - x2 * s, x2 * c + x1 * s], axis=-1).astype(x.dtype)


def apply_axial_rope(x, T):
    t = jnp.arange(T)
    half = x.shape[-1] // 2
    cr, sr = rope_angles(t // GRID_W, half, AXIAL_THETA)
    cc, sc = rope_angles(t % GRID_W, half, AXIAL_THETA)
    return jnp.concatenate([apply_rope(x[..., :half], cr, sr),
                            apply_rope(x[..., half:], cc, sc)], axis=-1)


def split_heads(z, n_heads, head_dim):
    B, T, _ = z.shape
    return z.reshape(B, T, n_heads, head_dim)


def neighbourhood_attention(q, k, v, rpb):
    B, T, H, dh = q.shape
    rows = T // GRID_W
    kr = min(NA_WIN_ROWS, rows)
    n_cb = GRID_W // NA_QCOLS
    qc = np.arange(GRID_W).reshape(n_cb, NA_QCOLS)
    band0 = np.clip(qc[:, 0] - NA_WIN_COLS // 2, 0, GRID_W - NA_KCOLS)
    kc = band0[:, None] + np.arange(NA_KCOLS)
    win0 = np.clip(qc - NA_WIN_COLS // 2, 0, GRID_W - NA_WIN_COLS)
    kcb = kc[:, None, :]
    col_mask = (kcb >= win0[..., None]) & (kcb < win0[..., None] + NA_WIN_COLS)
    dcol = np.clip(kcb - qc[..., None] + (NA_WIN_COLS - 1), 0, 2 * NA_WIN_COLS - 2)
    rpb_col = rpb[:, :, dcol]
    mask6 = jnp.asarray(col_mask)[None, None, :, :, None, :]
    qg = q.reshape(B, rows, GRID_W, H, dh)
    kg = k.reshape(B, rows, GRID_W, H, dh)
    vg = v.reshape(B, rows, GRID_W, H, dh)
    scale = dh ** -0.5

    def one_row(r):
        r0 = jnp.clip(r - kr // 2, 0, rows - kr)
        q_r = lax.dynamic_index_in_dim(qg, r, axis=1, keepdims=False).reshape(B, n_cb, NA_QCOLS, H, dh)
        k_r = lax.dynamic_slice_in_dim(kg, r0, kr, axis=1)[:, :, kc]
        v_r = lax.dynamic_slice_in_dim(vg, r0, kr, axis=1)[:, :, kc]
        s = jnp.einsum('bnqhd,brnkhd->bhnqrk', q_r, k_r).astype(jnp.float32) * scale
        drow = r0 + jnp.arange(kr) - r + (NA_WIN_ROWS - 1)
        bias = jnp.take(rpb_col, drow, axis=1).transpose(0, 2, 3, 1, 4)
        s = jnp.where(mask6, s + bias[None].astype(jnp.float32), -jnp.inf)
        p = jax.nn.softmax(s.reshape(B, H, n_cb, NA_QCOLS, kr * NA_KCOLS), axis=-1)
        p = p.reshape(B, H, n_cb, NA_QCOLS, kr, NA_KCOLS).astype(v.dtype)
        o = jnp.einsum('bhnqrk,brnkhd->bnqhd', p, v_r)
        return o.reshape(B, GRID_W, H, dh)

    out = lax.map(one_row, jnp.arange(rows))
    return out.transpose(1, 0, 2, 3, 4).reshape(B, T, H * dh)


def gqa_attention(q, k, v):
    B, T, Hq, dh = q.shape
    Hkv = k.shape[2]
    G = Hq // Hkv
    nblk = T // Q_BLOCK
    scale = dh ** -0.5
    qb = q.reshape(B, nblk, Q_BLOCK, Hkv, G, dh).transpose(1, 0, 2, 3, 4, 5)

    def one_block(q_blk):
        s = jnp.einsum('bqhgd,bkhd->bhgqk', q_blk, k).astype(jnp.float32) * scale
        p = jax.nn.softmax(s, axis=-1).astype(v.dtype)
        return jnp.einsum('bhgqk,bkhd->bqhgd', p, v)

    out = lax.map(one_block, qb)
    return out.transpose(1, 0, 2, 3, 4, 5).reshape(B, T, Hq * dh)


def diff_attention(q, k, v, lam):
    B, T, H2, dh = q.shape
    H = H2 // 2
    nblk = T // Q_BLOCK
    scale = dh ** -0.5
    qb = q.reshape(B, nblk, Q_BLOCK, H, 2, dh).transpose(1, 0, 2, 3, 4, 5)
    kk = k.reshape(B, T, H, 2, dh)

    def one_block(q_blk):
        s = jnp.einsum('bqhcd,bkhcd->bhcqk', q_blk, kk).astype(jnp.float32) * scale
        p = jax.nn.softmax(s, axis=-1)
        a = (p[:, :, 0] - lam * p[:, :, 1]).astype(v.dtype)
        return jnp.einsum('bhqk,bkhe->bqhe', a, v)

    out = lax.map(one_block, qb)
    return out.transpose(1, 0, 2, 3, 4).reshape(B, T, H, 2 * dh)


def encoder_trunk(x, ln_mix_e, w_in_e, rpb, q_norm_b, k_norm_b, w_out_e,
                  ln_mix_o, w_in_o, lambda_q1, lambda_k1, lambda_q2, lambda_k2, subln_g, w_out_o,
                  ln_mlp, w_up, w_down, ln_f):
    B, T, _ = x.shape
    cos1, sin1 = rope_angles(jnp.arange(T), HEAD_DIM, ROPE_THETA)
    for layer in range(DEPTH):
        j = layer // 2
        if layer % 2 == 0:
            h = rmsnorm(x, ln_mix_e[j])
            proj = h @ w_in_e[j]
            qa, ka, va, qb, kb, vb = jnp.split(proj, EVEN_SPLITS, axis=-1)
            a_out = neighbourhood_attention(split_heads(qa, NA_HEADS, HEAD_DIM),
                                            split_heads(ka, NA_HEADS, HEAD_DIM),
                                            split_heads(va, NA_HEADS, HEAD_DIM), rpb[j])
            qb = apply_axial_rope(rmsnorm(split_heads(qb, GQA_Q_HEADS, HEAD_DIM), q_norm_b[j], QK_NORM_EPS), T)
            kb = apply_axial_rope(rmsnorm(split_heads(kb, GQA_KV_HEADS, HEAD_DIM), k_norm_b[j], QK_NORM_EPS), T)
            b_out = gqa_attention(qb, kb, split_heads(vb, GQA_KV_HEADS, HEAD_DIM))
            x = x + jnp.concatenate([a_out, b_out], axis=-1) @ w_out_e[j]
        else:
            h = rmsnorm(x, ln_mix_o[j])
            proj = h @ w_in_o[j]
            qc, kc, vc = jnp.split(proj, ODD_SPLITS, axis=-1)
            qc = apply_rope(split_heads(qc, 2 * DIFF_HEADS, HEAD_DIM), cos1, sin1)
            kc = apply_rope(split_heads(kc, 2 * DIFF_HEADS, HEAD_DIM), cos1, sin1)
            vc = split_heads(vc, DIFF_HEADS, 2 * HEAD_DIM)
            lam_init = 0.8 - 0.6 * math.exp(-0.3 * layer)
            lam = (jnp.exp(jnp.sum(lambda_q1[j].astype(jnp.float32) * lambda_k1[j].astype(jnp.float32)))
                   - jnp.exp(jnp.sum(lambda_q2[j].astype(jnp.float32) * lambda_k2[j].astype(jnp.float32)))
                   + lam_init)
            o = diff_attention(qc, kc, vc, lam)
            o = rmsnorm(o, subln_g[j], SUBLN_EPS) * (1.0 - lam_init)
            x = x + o.reshape(B, T, DIFF_WIDTH) @ w_out_o[j]
        h = rmsnorm(x, ln_mlp[layer])
        x = x + jnp.square(jax.nn.relu(h @ w_up[layer])) @ w_down[layer]
    return rmsnorm(x, ln_f)


def setup_inputs(seed: int = 0) -> dict:
    key = jax.random.key(seed)
    ks = jax.random.split(key, 24)
    f32 = jnp.float32

    def normal(k, shape, scale):
        return jax.random.normal(k, shape, dtype=f32) * scale

    def gain(k, shape):
        return 1.0 + 0.02 * jax.random.normal(k, shape, dtype=f32)

    return {
        "x_prompt": normal(ks[0], (BATCH, SEQ, D_MODEL), 1.0),
        "x_sample": normal(ks[1], (DEC_BATCH, DEC_SEQ, D_MODEL), 1.0),
        "ln_mix_e": gain(ks[2], (N_EVEN, D_MODEL)),
        "w_in_e": normal(ks[3], (N_EVEN, D_MODEL, EVEN_IN), D_MODEL ** -0.5),
        "rpb": normal(ks[4], (N_EVEN, NA_HEADS, 2 * NA_WIN_ROWS - 1, 2 * NA_WIN_COLS - 1), 0.02),
        "q_norm_b": gain(ks[5], (N_EVEN, HEAD_DIM)),
        "k_norm_b": gain(ks[6], (N_EVEN, HEAD_DIM)),
        "w_out_e": normal(ks[7], (N_EVEN, EVEN_OUT, D_MODEL), EVEN_OUT ** -0.5),
        "ln_mix_o": gain(ks[8], (N_ODD, D_MODEL)),
        "w_in_o": normal(ks[9], (N_ODD, D_MODEL, ODD_IN), D_MODEL ** -0.5),
        "lambda_q1": normal(ks[10], (N_ODD, HEAD_DIM), 0.1),
        "lambda_k1": normal(ks[11], (N_ODD, HEAD_DIM), 0.1),
        "lambda_q2": normal(ks[12], (N_ODD, HEAD_DIM), 0.1),
        "lambda_k2": normal(ks[13], (N_ODD, HEAD_DIM), 0.1),
        "subln_g": gain(ks[14], (N_ODD, 2 * HEAD_DIM)),
        "w_out_o": normal(ks[15], (N_ODD, DIFF_WIDTH, D_MODEL), DIFF_WIDTH ** -0.5),
        "ln_mlp": gain(ks[16], (DEPTH, D_MODEL)),
        "w_up": normal(ks[17], (DEPTH, D_MODEL, D_FF), D_MODEL ** -0.5),
        "w_down": normal(ks[18], (DEPTH, D_FF, D_MODEL), D_FF ** -0.5),
        "ln_f": gain(ks[19], (D_MODEL,)),
    }


def reference(x_prompt, x_sample, ln_mix_e, w_in_e, rpb, q_norm_b, k_norm_b, w_out_e,
              ln_mix_o, w_in_o, lambda_q1, lambda_k1, lambda_q2, lambda_k2, subln_g, w_out_o,
              ln_mlp, w_up, w_down, ln_f):
    y_prompt = encoder_trunk(x_prompt, ln_mix_e, w_in_e, rpb, q_norm_b, k_norm_b, w_out_e,
                             ln_mix_o, w_in_o, lambda_q1, lambda_k1, lambda_q2, lambda_k2, subln_g, w_out_o,
                             ln_mlp, w_up, w_down, ln_f)
    y_sample = encoder_trunk(x_sample, ln_mix_e, w_in_e, rpb, q_norm_b, k_norm_b, w_out_e,
                             ln_mix_o, w_in_o, lambda_q1, lambda_k1, lambda_q2, lambda_k2, subln_g, w_out_o,
                             ln_mlp, w_up, w_down, ln_f)
    return (y_prompt, y_sample)
```

```python
import math
from contextlib import ExitStack
import numpy as np
import ml_dtypes
import concourse.bass as bass
import concourse.mybir as mybir
from concourse.bass_utils import run_bass_kernel_spmd

F32 = mybir.dt.float32
BF16 = mybir.dt.bfloat16
AF = mybir.ActivationFunctionType
ALU = mybir.AluOpType
AX = mybir.AxisListType

D = 1024
NEG = -30000.0
LAM_INIT1 = 0.8 - 0.6 * math.exp(-0.3 * 1)
SAME_ENGINE_SYNC = True


class Buf:
    __slots__ = ("w", "r", "name")

    def __init__(self, name=""):
        self.w = None
        self.r = {}
        self.name = name


class Eng:
    def __init__(self, name, attr):
        self.name = name
        self.attr = attr
        self.items = []
        self.seen = {}
        self.sem = None
        self.cnt = 0


class DSem:
    def __init__(self, sem):
        self.sem = sem
        self.cnt = 0


class K:
    def __init__(self, nc, sems):
        self.nc = nc
        self.sems = sems
        self.sem_i = 0
        self.PE = Eng("pe", "tensor")
        self.ACT = Eng("act", "scalar")
        self.DVE = Eng("dve", "vector")
        self.POOL = Eng("pool", "gpsimd")
        self.SP = Eng("sp", "sync")
        self.engs = [self.PE, self.ACT, self.DVE, self.POOL, self.SP]
        self.store_toks = []

    def new_sem(self):
        s = self.sems[self.sem_i]
        self.sem_i += 1
        return s

    def begin_phase(self):
        for e in self.engs:
            e.items = []
            e.seen = {}
            e.sem = self.new_sem() if e.name != "sp" else None
            e.cnt = 0
        self.store_toks = []

    def dsem(self):
        return DSem(self.new_sem())

    def _deps(self, E, reads, writes):
        waits = []

        def need(s, v, raw):
            if s is E.sem:
                if E.name == "pe" or not SAME_ENGINE_SYNC or not raw:
                    return
            if E.seen.get(id(s), 0) >= v:
                return
            E.seen[id(s)] = v
            waits.append((s, v))

        for b in reads:
            if b.w is not None:
                need(*b.w, True)
        for b in writes:
            if b.w is not None:
                need(*b.w, False)
            for s, v in b.r.values():
                need(s, v, False)
        return waits

    def _commit(self, tok, reads, writes):
        for b in reads:
            b.r[id(tok[0])] = tok
        for b in writes:
            b.w = tok
            b.r = {}

    def op(self, E, fn, reads=(), writes=(), inc=True):
        waits = self._deps(E, reads, writes)
        if inc:
            E.cnt += 1
            tok = (E.sem, E.cnt)
        else:
            tok = (E.sem, E.cnt + 1)
        sem = E.sem

        def run(eng, fn=fn, waits=waits, inc=inc, sem=sem):
            for s, v in waits:
                eng.wait_ge(s, v)
            ins = fn(eng)
            if inc:
                ins.then_inc(sem, 1)

        E.items.append(run)
        self._commit(tok, reads, writes)
        return tok

    def dma(self, E, out, in_, ds, reads=(), writes=(), store=False):
        waits = self._deps(E, reads, writes)
        ds.cnt += 1
        tok = (ds.sem, ds.cnt * 16)

        def run(eng, waits=waits, out=out, in_=in_, sem=ds.sem):
            for s, v in waits:
                eng.wait_ge(s, v)
            eng.dma_start(out=out, in_=in_).then_inc(sem, 16)

        E.items.append(run)
        self._commit(tok, reads, writes)
        if store:
            self.store_toks.append(tok)
        return tok

    def seal(self, ds, bufs):
        for b in bufs:
            b.w = (ds.sem, ds.cnt * 16)

    def end_phase(self):
        final = {}
        for s, v in self.store_toks:
            if final.get(id(s), (s, 0))[1] < v:
                final[id(s)] = (s, v)
        fin = list(final.values())

        def run(eng, fin=fin):
            for s, v in fin:
                eng.wait_ge(s, v)

        self.SP.items.append(run)
        nc = self.nc
        with nc.Block() as block:
            @block.tensor
            def _(eng):
                for it in self.PE.items:
                    it(eng)

            @block.scalar
            def _(eng):
                for it in self.ACT.items:
                    it(eng)

            @block.vector
            def _(eng):
                for it in self.DVE.items:
                    it(eng)

            @block.gpsimd
            def _(eng):
                for it in self.POOL.items:
                    it(eng)

            @block.sync
            def _(eng):
                for it in self.SP.items:
                    it(eng)


class Ring:
    def __init__(self, aps, name="ring"):
        self.aps = aps
        self.bufs = [Buf(f"{name}{i}") for i in range(len(aps))]
        self.i = 0

    def next(self):
        j = self.i % len(self.aps)
        self.i += 1
        return self.aps[j], self.bufs[j]


def _rope_tables():
    f32 = np.float32

    def angles(pos, dim, theta=10000.0):
        inv = (1.0 / np.power(f32(theta), np.arange(0, dim, 2, dtype=f32) / f32(dim))).astype(f32)
        ang = pos.astype(f32)[:, None] * inv[None, :]
        return np.cos(ang).astype(f32), np.sin(ang).astype(f32)

    t = np.arange(4096)
    cr, sr = angles(t // 64, 32)
    cc, sc = angles(t % 64, 32)
    cosA = np.concatenate([cr, cr, cc, cc], axis=1)
    sinA = np.concatenate([-sr, sr, -sc, sc], axis=1)
    c1, s1 = angles(t, 64)
    cos1 = np.concatenate([c1, c1], axis=1)
    sin1 = np.concatenate([-s1, s1], axis=1)
    tabs = np.stack([cosA * 0.125, sinA * 0.125, cosA, sinA, cos1 * 0.125, sin1 * 0.125, cos1, sin1]).astype(f32)
    tabs = tabs.reshape(8, 32, 128, 64).transpose(0, 2, 1, 3).reshape(8, 128, 32 * 64)
    return np.ascontiguousarray(tabs)


def _na_consts(rpb):
    a = np.arange(2)[:, None, None, None]
    kc = np.arange(64)[None, :, None, None]
    j = np.arange(14)[None, None, :, None]
    qc = np.arange(64)[None, None, None, :]
    drow = (6 - j) + a + 7 + 0 * kc + 0 * qc
    dcol = np.clip(kc - qc + 15, 0, 30) + 0 * a + 0 * j
    g = rpb[:, drow, dcol]
    g = g.transpose(1, 2, 0, 3, 4).reshape(128, 8, 7, 2, 64).transpose(0, 1, 3, 2, 4)
    g = np.ascontiguousarray(g).astype(np.float32)
    win0 = np.clip(qc - 8, 0, 48)
    ok = (kc >= win0) & (kc < win0 + 16)
    m = np.where(ok, 0.0, NEG).astype(np.float32) + 0 * a + 0 * j
    m = np.broadcast_to(m, (2, 64, 14, 64)).reshape(128, 7, 2, 64).transpose(0, 2, 1, 3)
    m = np.ascontiguousarray(m).astype(np.float32)
    return g, m


def _na_units(qr0, R):
    by = {}
    for i in range(8):
        qr = qr0 + i
        r0 = min(max(qr - 4, 0), R - 8)
        for m in range(4):
            by.setdefault((r0 + 2 * m, i % 2), []).append(i // 2)
    units = []
    for (ks, par) in sorted(by):
        lst = sorted(by[(ks, par)])
        while lst:
            n = 1
            while n < len(lst) and lst[n] == lst[n - 1] + 1:
                n += 1
            r2 = lst[0]
            j = 6 - ks + qr0 + 2 * r2 + par
            assert 0 <= j and j + 2 * (n - 1) <= 13
            units.append((ks, par * 4 + r2, n, j % 2, j // 2))
            lst = lst[n:]
    return units


def build(seq_lens, nphases=6, debug=False):
    Ttot = sum(seq_lens)
    nc = bass.Bass("TRN2", target_bir_lowering=False)

    def din(name, shape, dt=F32):
        return nc.dram_tensor(name, list(shape), dt, kind="ExternalInput").ap()

    def dscr(name, shape, dt):
        return nc.dram_tensor(name, list(shape), dt, kind=("ExternalOutput" if debug else "Internal")).ap()

    xin = din("xin", [Ttot, D])
    w_in_e = din("w_in_e", [D, 2304])
    w_out_e = din("w_out_e", [D, D])
    w_in_o = din("w_in_o", [D, 3072])
    w_out_o = din("w_out_o", [D, D])
    w_up = din("w_up", [2, D, 4096])
    w_down = din("w_down", [2, 4096, D])
    gains = din("gains", [5, D])
    qkn = din("qkn", [2, 64])
    lamv = din("lamv", [4, 64])
    subg = din("subg", [128, 1])
    rpbG = din("rpbG", [128, 8, 2, 7, 64])
    naMask = din("naMask", [128, 2, 7, 64])
    ropeT = din("ropeT", [8, 128, 32 * 64])
    identF = din("identF", [128, 128])
    y = nc.dram_tensor("y", [Ttot, D], F32, kind="ExternalOutput").ap()

    qk0T = dscr("qk0T", [13 * 128, Ttot], BF16)
    v0 = dscr("v0", [Ttot, 640], BF16)
    x1 = dscr("x1", [Ttot, D], F32)
    x2 = dscr("x2", [Ttot, D], F32)
    qk1T = dscr("qk1T", [16 * 128, Ttot], BF16)
    v1 = dscr("v1", [Ttot, D], BF16)
    x3 = dscr("x3", [Ttot, D], F32)
    wupb = nc.dram_tensor("wupb", [2, D, 4096], BF16, kind="Internal").ap()
    wdnb = nc.dram_tensor("wdnb", [2, 4096, D], BF16, kind="Internal").ap()
    winob = nc.dram_tensor("winob", [D, 3072], BF16, kind="Internal").ap()
    woutob = nc.dram_tensor("woutob", [D, D], BF16, kind="Internal").ap()

    seq_base = [sum(seq_lens[:i]) for i in range(len(seq_lens))]
    tiles128 = []
    for s, T in enumerate(seq_lens):
        for tt in range(T // 128):
            tiles128.append((seq_base[s] + tt * 128, tt))

    with ExitStack() as gstack:
        sems = [gstack.enter_context(nc.semaphore(f"s{i}")) for i in range(100)]
        k = K(nc, sems)
        PE, ACT, DVE, POOL, SP = k.PE, k.ACT, k.DVE, k.POOL, k.SP

        uid = [0]

        def sb(st, name, shape, dt):
            uid[0] += 1
            return st.enter_context(nc.sbuf_tensor(f"sb{uid[0]}_{name}", list(shape), dt))

        def ps(st, name, shape, dt):
            uid[0] += 1
            return st.enter_context(nc.psum_tensor(f"ps{uid[0]}_{name}", list(shape), dt))

        def load_ident(st, dsm):
            idf = sb(st, "idf", [128, 128], F32)
            idb = sb(st, "idb", [128, 128], BF16)
            b1, b2 = Buf(), Buf()
            k.dma(SP, idf[:], identF[:, :], dsm, writes=[b1])

            def cast():
                k.seal(dsm, [b1])
                k.op(DVE, lambda e: e.tensor_copy(out=idb[:], in_=idf[:]), reads=[b1], writes=[b2])
            return idb, b2, cast

        def mk_col(st, val):
            t = sb(st, "col", [128, 1], F32)
            b = Buf()
            k.op(DVE, lambda e: e.memset(t[:], val), writes=[b])
            return t, b

        def load_bc(st, name, src_row, n, dsm):
            t = sb(st, name, [128, n], F32)
            b = Buf(name)
            k.dma(SP, t[:], src_row.partition_broadcast(128), dsm, writes=[b])
            return t, b

        def pool_join(st, bufs, wbuf):
            jt = sb(st, "join", [128, 1], F32)
            k.op(POOL, lambda e: e.memset(jt[:], 0.0), reads=bufs, writes=[wbuf])

        def load_w_cast(st, wsb, wbuf, src, nchunks, dsm, after=None):
            v = src.rearrange("(c p) f -> p c f", p=128)
            ds2 = [dsm, k.dsem()]
            thr = [Buf(), Buf()]
            for c in range(nchunks):
                k.dma(POOL, wsb[:, c, :], v[:, c, :], ds2[c % 2], writes=[thr[c % 2]],
                      reads=([after] if (after is not None and c == 0) else []))
            pool_join(st, thr, wbuf)

        def precast(pairs, dsm, after=None):
            ds2 = [dsm, k.dsem()]
            thr = [Buf(), Buf()]
            j = 0
            for src, dst in pairs:
                vs = src.rearrange("(c p) f -> p c f", p=128)
                vd_ = dst.rearrange("(c p) f -> p c f", p=128)
                nchunk = vs.shape[1]
                step = max(1, 4096 // vs.shape[2])
                for c in range(0, nchunk, step):
                    k.dma(POOL, vd_[:, c:c + step, :], vs[:, c:c + step, :], ds2[j % 2], writes=[thr[j % 2]], store=True,
                          reads=([after] if (after is not None and j == 0) else []))
                    j += 1

        def load_w_bf16(wsb, wbuf, src, nchunks, dsm, step=1):
            v = src.rearrange("(c p) f -> p c f", p=128)
            for c in range(0, nchunks, step):
                k.dma(SP, wsb[:, c:c + step, :], v[:, c:c + step, :], dsm, writes=[wbuf])

        cur_eps = {}

        def rms_part1(xt, xb, g_bc, gb, junk, junkb, ssr, rsr, h_ring):
            ss, ssb = ssr.next()
            rs, rsb = rsr.next()
            h, hb = h_ring.next()
            k.op(ACT, lambda e: e.activation(out=junk[:], in_=xt, func=AF.Square, scale=1.0 / 32, accum_out=ss[:]),
                 reads=[xb], writes=[junkb, ssb])
            ep, epb = cur_eps["e6"]
            k.op(ACT, lambda e: e.activation(out=rs[:], in_=ss[:], func=AF.Sqrt, bias=ep[:], scale=1.0),
                 reads=[ssb, epb], writes=[rsb])
            k.op(DVE, lambda e: e.reciprocal(out=rs[:], in_=rs[:]), reads=[rsb], writes=[rsb])
            k.op(DVE, lambda e: e.scalar_tensor_tensor(out=h[:], in0=xt, scalar=rs[:, 0:1], in1=g_bc,
                                                       op0=ALU.mult, op1=ALU.mult),
                 reads=[xb, rsb, gb], writes=[hb])
            return h, hb

        def rms_part2(h, hb, idb, idbuf, trp, trpb, hT_ap, hTb):
            for c in range(8):
                k.op(PE, lambda e, c=c: e.transpose(out=trp[:, c * 128:(c + 1) * 128], in_=h[:, c * 128:(c + 1) * 128],
                                                    identity=idb[:]),
                     reads=[hb, idbuf], writes=[trpb], inc=(c == 7))
            k.op(ACT, lambda e: e.copy(out=hT_ap, in_=trp[:].rearrange("p (c t) -> p c t", c=8)),
                 reads=[trpb], writes=[hTb])

        def rope_evac_gen(src, srcb, width, nH, S, hs, cosT, sinT, tabb, tmp, tmpb, u, ub, dst, dstb, src_is_psum=True):
            def v5(ap):
                return ap.rearrange("p (h s t i) -> p h s t i", h=nH, s=S, t=2, i=hs)

            def tb(ap):
                return ap.unsqueeze(1).to_broadcast([128, nH, 64])

            def tb5(ap, half):
                a5 = ap.rearrange("p (s t i) -> p s t i", s=S, t=2, i=hs)[:, :, half, :]
                return a5.unsqueeze(1).to_broadcast([128, nH, S, hs])

            e1 = DVE
            k.op(e1, lambda e: e.tensor_tensor(out=tmp[:, 0:width].rearrange("p (h d) -> p h d", h=nH),
                                               in0=src.rearrange("p (h d) -> p h d", h=nH),
                                               in1=tb(cosT), op=ALU.mult),
                 reads=[srcb, tabb], writes=[tmpb])
            yield
            k.op(e1, lambda e: e.tensor_tensor(out=v5(u[:, 0:width])[:, :, :, 0, :], in0=v5(src)[:, :, :, 1, :],
                                               in1=tb5(sinT, 0), op=ALU.mult),
                 reads=[srcb, tabb], writes=[ub])
            yield
            k.op(e1, lambda e: e.tensor_tensor(out=v5(u[:, 0:width])[:, :, :, 1, :], in0=v5(src)[:, :, :, 0, :],
                                               in1=tb5(sinT, 1), op=ALU.mult),
                 reads=[srcb, tabb], writes=[ub])
            yield
            k.op(POOL, lambda e: e.tensor_tensor(out=dst, in0=tmp[:, 0:width], in1=u[:, 0:width], op=ALU.add),
                 reads=[tmpb, ub], writes=[dstb])
            yield

        def rope_evac(*a, **kw):
            for _ in rope_evac_gen(*a, **kw):
                pass

        def run_interleaved(gens):
            gens = list(gens)
            while gens:
                for g in list(gens):
                    try:
                        next(g)
                    except StopIteration:
                        gens.remove(g)

        def phase_inproj(layer):
            k.begin_phase()
            if layer == 0:
                w_src, F, gi = w_in_e, 2304, 0
                xsrc, qkT_d, v_d, nchunk, vw = xin, qk0T, v0, 13, 640
                tab0 = 0
            else:
                w_src, F, gi = w_in_o, 3072, 2
                xsrc, qkT_d, v_d, nchunk, vw = x2, qk1T, v1, 16, 1024
                tab0 = 4
            qkw = nchunk * 128
            with ExitStack() as st:
                cur_eps["e6"] = mk_col(st, 1e-6)
                dW, dC = k.dsem(), k.dsem()
                dS1 = [k.dsem() for _ in range(2)]
                dS2 = [k.dsem() for _ in range(2)]
                dXs = [k.dsem() for _ in range(3)]
                wsb = sb(st, "wsb", [128, 8, F], BF16)
                wb = Buf("w")
                if layer == 0:
                    load_w_cast(st, wsb, wb, w_src, 8, dW)
                else:
                    load_w_bf16(wsb, wb, winob, 8, dW)
                idb, idbuf, idcast = load_ident(st, dC)
                g_bc, gb = load_bc(st, "g_bc", gains[gi, :], D, dC)
                TT = 32
                tabs = sb(st, "tabs", [128, 4, TT, 64], F32)
                tabb = Buf("tabs")
                for i in range(4):
                    k.dma(SP, tabs[:, i, :, :].rearrange("p t d -> p (t d)"), ropeT[tab0 + i], dC, writes=[tabb])
                cb = [gb, tabb]
                if layer == 0:
                    gq, gqb = load_bc(st, "gq", qkn[0, :], 64, dC)
                    gk, gkb = load_bc(st, "gk", qkn[1, :], 64, dC)
                    cb += [gqb, gkb]
                k.seal(dC, cb)
                idcast()
                xr = Ring([sb(st, f"x{i}", [128, D], F32) for i in range(3)], "x")
                junk = sb(st, "junk", [128, D], BF16)
                junkb = Buf()
                ssr = Ring([sb(st, f"ss{i}", [128, 1], F32) for i in range(2)])
                rsr = Ring([sb(st, f"rs{i}", [128, 1], F32) for i in range(2)])
                hr = Ring([sb(st, f"h{i}", [128, D], BF16) for i in range(2)])
                hTr = Ring([sb(st, f"hT{i}", [128, 8, 128], BF16) for i in range(2)])
                qkr = Ring([sb(st, f"qktm{i}", [128, qkw], BF16) for i in range(2)])
                vst = Ring([sb(st, f"vst{i}", [128, 4, vw], BF16) for i in range(2)])
                qst = Ring([sb(st, f"qst{i}", [128, nchunk, 512], BF16) for i in range(2)])
                tmpr = Ring([sb(st, f"tmp{i}", [128, 512], F32) for i in range(2)])
                ur = Ring([sb(st, f"u{i}", [128, 512], F32) for i in range(2)])
                if layer == 0:
                    sqr = Ring([sb(st, f"sq{i}", [128, 512], F32) for i in range(2)])
                    xnr = Ring([sb(st, f"xn{i}", [128, 512], F32) for i in range(2)])
                    ss8r = Ring([sb(st, f"ss8{i}", [128, 8], F32) for i in range(2)])
                trh = ps(st, "trh", [128, 1024], BF16); trhb = Buf()
                trq = Ring([ps(st, f"trq{i}", [128, 1024], BF16) for i in range(2)])
                mmr = Ring([ps(st, f"mm{i}", [128, 512], F32) for i in range(4)])

                ntiles = len(tiles128)
                xt_list = {}

                def issue_load(i):
                    if i >= ntiles:
                        return
                    t0, tt = tiles128[i]
                    xt, xb = xr.next()
                    k.dma(SP, xt[:], xsrc[t0:t0 + 128, :], dXs[i % 3], writes=[xb])
                    xt_list[i] = (xt, xb)

                tstate = {}

                def stage1(i):
                    if i >= ntiles:
                        return
                    issue_load(i + 2)
                    xt, xb = xt_list.pop(i)
                    h, hb = rms_part1(xt[:], xb, g_bc[:], gb, junk, junkb, ssr, rsr, hr)
                    tstate[i] = {"h": h, "hb": hb}

                def stageT(i):
                    if i >= ntiles:
                        return
                    hT, hTb = hTr.next()
                    rms_part2(tstate[i]["h"], tstate[i]["hb"], idb, idbuf, trh, trhb, hT[:], hTb)
                    tstate[i]["hT"] = (hT, hTb)

                cur = {"v": None, "q": None}

                def stageMM(i):
                    t0, tt = tiles128[i]
                    ti = i % 4
                    if ti == 0:
                        cur["v"] = vst.next()
                        cur["q"] = qst.next()
                    vS, vSb = cur["v"]
                    qS, qSb = cur["q"]
                    hT, hTb = tstate[i]["hT"]
                    qk, qkb = qkr.next()
                    nft = (F + 511) // 512
                    chains = []
                    for ft in range(nft):
                        f0 = ft * 512
                        fw = min(512, F - f0)
                        mp, mpb = mmr.next()
                        for c in range(8):
                            k.op(PE, lambda e, c=c, mp=mp, f0=f0, fw=fw, hT=hT: e.matmul(
                                mp[:, 0:fw], hT[:, c, :], wsb[:, c, f0:f0 + fw], start=(c == 0), stop=(c == 7)),
                                reads=[hTb, wb], writes=[mpb], inc=(c == 7))
                        if layer == 0:
                            if ft == 0:
                                k.op(ACT, lambda e, mp=mp, qk=qk: e.activation(out=qk[:, 0:512], in_=mp[:], func=AF.Copy, scale=0.125),
                                     reads=[mpb], writes=[qkb])
                            elif ft == 1:
                                k.op(ACT, lambda e, mp=mp, qk=qk: e.copy(out=qk[:, 512:1024], in_=mp[:]),
                                     reads=[mpb], writes=[qkb])
                            elif ft == 2:
                                k.op(ACT, lambda e, mp=mp, vS=vS, ti=ti: e.copy(out=vS[:, ti, 0:512], in_=mp[:]),
                                     reads=[mpb], writes=[vSb])
                            else:
                                if ft == 3:
                                    nH, gt, gtb, ctab, stab, dcol = 8, gq, gqb, 0, 1, 1024
                                else:
                                    nH, gt, gtb, ctab, stab, dcol = 2, gk, gkb, 2, 3, 1536
                                    k.op(ACT, lambda e, mp=mp, vS=vS, ti=ti: e.copy(out=vS[:, ti, 512:640], in_=mp[:, 128:256]),
                                         reads=[mpb], writes=[vSb])
                                def qkn_chain(mp=mp, mpb=mpb, nH=nH, gt=gt, gtb=gtb, ctab=ctab, stab=stab, dcol=dcol, qk=qk, qkb=qkb, tt=tt):
                                    wdt = nH * 64
                                    sq, sqb = sqr.next()
                                    xn, xnb = xnr.next()
                                    ss8, ss8b = ss8r.next()
                                    tmp, tmpb = tmpr.next()
                                    u, ub = ur.next()
                                    k.op(ACT, lambda e, mp=mp, wdt=wdt, sq=sq: e.activation(out=sq[:, 0:wdt], in_=mp[:, 0:wdt], func=AF.Square, scale=0.125),
                                         reads=[mpb], writes=[sqb])
                                    yield
                                    k.op(DVE, lambda e, wdt=wdt, nH=nH, ss8=ss8, sq=sq: e.tensor_reduce(
                                        out=ss8[:, 0:nH], in_=sq[:, 0:wdt].rearrange("p (h d) -> p h d", h=nH), axis=AX.X, op=ALU.add),
                                        reads=[sqb], writes=[ss8b])
                                    yield
                                    k.op(ACT, lambda e, nH=nH, ss8=ss8: e.activation(out=ss8[:, 0:nH], in_=ss8[:, 0:nH], func=AF.Sqrt,
                                                                            bias=cur_eps["e6"][0][:], scale=1.0),
                                         reads=[ss8b, cur_eps["e6"][1]], writes=[ss8b])
                                    yield
                                    k.op(DVE, lambda e, nH=nH, ss8=ss8: e.reciprocal(out=ss8[:, 0:nH], in_=ss8[:, 0:nH]),
                                         reads=[ss8b], writes=[ss8b])
                                    yield
                                    k.op(DVE, lambda e, mp=mp, wdt=wdt, nH=nH, xn=xn, ss8=ss8: e.tensor_tensor(
                                        out=xn[:, 0:wdt].rearrange("p (h d) -> p h d", h=nH),
                                        in0=mp[:, 0:wdt].rearrange("p (h d) -> p h d", h=nH),
                                        in1=ss8[:, 0:nH].unsqueeze(2).to_broadcast([128, nH, 64]), op=ALU.mult),
                                        reads=[mpb, ss8b], writes=[xnb])
                                    yield
                                    k.op(POOL, lambda e, wdt=wdt, nH=nH, gt=gt, xn=xn: e.tensor_tensor(
                                        out=xn[:, 0:wdt].rearrange("p (h d) -> p h d", h=nH),
                                        in0=xn[:, 0:wdt].rearrange("p (h d) -> p h d", h=nH),
                                        in1=gt[:].unsqueeze(1).to_broadcast([128, nH, 64]), op=ALU.mult),
                                        reads=[xnb, gtb], writes=[xnb])
                                    yield
                                    yield from rope_evac_gen(xn[:, 0:wdt], xnb, wdt, nH, 2, 16, tabs[:, ctab, tt, :], tabs[:, stab, tt, :], tabb,
                                              tmp, tmpb, u, ub, qk[:, dcol:dcol + wdt], qkb)
                                chains.append(qkn_chain())
                        else:
                            if ft < 4:
                                ctab, stab = (0, 1) if ft < 2 else (2, 3)
                                tmp, tmpb = tmpr.next()
                                u, ub = ur.next()
                                rope_evac(mp[:], mpb, 512, 8, 1, 32, tabs[:, ctab, tt, :], tabs[:, stab, tt, :], tabb,
                                          tmp, tmpb, u, ub, qk[:, f0:f0 + 512], qkb)
                            else:
                                k.op(ACT, lambda e, mp=mp, vS=vS, ti=ti, f0=f0: e.copy(out=vS[:, ti, f0 - 2048:f0 - 2048 + 512], in_=mp[:]),
                                     reads=[mpb], writes=[vSb])
                    run_interleaved(chains)
                    tstate[i].update(qk=qk, qkb=qkb, qS=qS, qSb=qSb, vS=vS, vSb=vSb)

                def stageQ(i):
                    if i < 0:
                        return
                    t0, tt = tiles128[i]
                    ti = i % 4
                    stt = tstate.pop(i)
                    qk, qkb, qS, qSb, vS, vSb = stt["qk"], stt["qkb"], stt["qS"], stt["qSb"], stt["vS"], stt["vSb"]
                    c0 = 0
                    while c0 < nchunk:
                        ncb = min(8, nchunk - c0)
                        tq, tqb = trq.next()
                        for c in range(ncb):
                            k.op(PE, lambda e, c=c, c0=c0, tq=tq, qk=qk: e.transpose(
                                out=tq[:, c * 128:(c + 1) * 128], in_=qk[:, (c0 + c) * 128:(c0 + c + 1) * 128], identity=idb[:]),
                                reads=[qkb, idbuf], writes=[tqb], inc=(c == ncb - 1))
                        if layer == 0:
                            k.op(DVE, lambda e, c0=c0, tq=tq, qS=qS, ti=ti: e.tensor_copy(
                                out=qS[:, c0:c0 + 4, :].rearrange("p c (par r q) -> p c par r q", par=2, r=4)[:, :, :, ti, :],
                                in_=tq[:, 0:512].rearrange("p (c par q) -> p c par q", c=4, par=2)),
                                reads=[tqb], writes=[qSb])
                            nk = ncb - 4
                            k.op(DVE, lambda e, c0=c0, nk=nk, tq=tq, qS=qS, ti=ti: e.tensor_copy(
                                out=qS[:, c0 + 4:c0 + 4 + nk, ti * 128:(ti + 1) * 128],
                                in_=tq[:, 512:512 + nk * 128].rearrange("p (c t) -> p c t", c=nk)),
                                reads=[tqb], writes=[qSb])
                        else:
                            k.op(DVE, lambda e, c0=c0, ncb=ncb, tq=tq, qS=qS, ti=ti: e.tensor_copy(
                                out=qS[:, c0:c0 + ncb, ti * 128:(ti + 1) * 128],
                                in_=tq[:, 0:ncb * 128].rearrange("p (c t) -> p c t", c=ncb)),
                                reads=[tqb], writes=[qSb])
                        c0 += ncb
                    if ti == 3:
                        tb0 = t0 - 384
                        sl = (i // 4) % 2
                        k.dma(SP, v_d[tb0:tb0 + 512, :].rearrange("(t p) f -> p t f", p=128), vS[:], dS1[sl],
                              reads=[vSb], store=True)
                        k.dma(SP, qkT_d.rearrange("(c p) t -> p c t", p=128)[:, :, tb0:tb0 + 512], qS[:], dS2[sl],
                              reads=[qSb], store=True)

                issue_load(0)
                issue_load(1)
                stage1(0)
                stageT(0)
                for i in range(ntiles):
                    stage1(i + 1)
                    stageMM(i)
                    stageT(i + 1)
                    stageQ(i - 1)
                stageQ(ntiles - 1)
                k.end_phase()

        def phase_mlp(layer):
            k.begin_phase()
            xsrc = x1 if layer == 0 else x3
            xdst = x2 if layer == 0 else y
            gi = 1 if layer == 0 else 3
            with ExitStack() as st:
                cur_eps["e6"] = mk_col(st, 1e-6)
                dW, dW2, dC = k.dsem(), k.dsem(), k.dsem()
                dS = [k.dsem() for _ in range(1)]
                dXs = [k.dsem() for _ in range(6)]
                wup = sb(st, "wup", [128, 8, 4096], BF16); wub = Buf()
                wdn = sb(st, "wdn", [128, 32, D], BF16); wdb = Buf()
                load_w_bf16(wup, wub, wupb[layer], 8, dW)
                load_w_bf16(wdn, wdb, wdnb[layer], 32, dW2, step=4)
                idb, idbuf, idcast = load_ident(st, dC)
                g_bc, gb = load_bc(st, "g_bc", gains[gi, :], D, dC)
                cb = [gb]
                if layer == 1:
                    gf, gfb = load_bc(st, "gf", gains[4, :], D, dC)
                    cb.append(gfb)
                k.seal(dC, cb)
                idcast()
                nstore = [0]
                xr = Ring([sb(st, f"x{i}", [128, D], F32) for i in range(6)], "x")
                junk = sb(st, "junk", [128, D], BF16); junkb = Buf()
                ssr = Ring([sb(st, f"ss{i}", [128, 1], F32) for i in range(4)])
                rsr = Ring([sb(st, f"rs{i}", [128, 1], F32) for i in range(4)])
                hr = Ring([sb(st, f"h{i}", [128, D], BF16) for i in range(4)])
                hTs = [sb(st, f"hT{i}", [128, 8, 256], BF16) for i in range(2)]
                hTbss = [[Buf(), Buf()], [Buf(), Buf()]]
                aT = sb(st, "aT", [128, 32, 256], BF16); aTb = Buf()
                rr = Ring([sb(st, f"r{i}", [128, 256], F32) for i in range(3)])
                xo = Ring([sb(st, f"xo{i}", [128, D], F32) for i in range(1)])
                trhr = Ring([ps(st, f"trh{i}", [128, 1024], BF16) for i in range(2)])
                upr = Ring([ps(st, f"up{i}", [128, 512], F32) for i in range(2)])
                dnr = Ring([ps(st, f"dn{i}", [128, 512], F32) for i in range(4)])
                ntiles = len(tiles128)
                ngroups = ntiles // 2
                xt_list = {}

                def issue_load(i):
                    if i >= ntiles:
                        return
                    t0, tt = tiles128[i]
                    xt, xb = xr.next()
                    k.dma(SP, xt[:], xsrc[t0:t0 + 128, :], dXs[i % 6], writes=[xb])
                    xt_list[i] = (xt, xb)

                groups = {}

                def norm1(g):
                    if g >= ngroups:
                        return
                    xs = []
                    for j in range(2):
                        i = g * 2 + j
                        xt, xb = xt_list.pop(i)
                        h, hb = rms_part1(xt[:], xb, g_bc[:], gb, junk, junkb, ssr, rsr, hr)
                        xs.append((xt, xb, tiles128[i][0], h, hb))
                    groups[g] = xs

                def norm2(g):
                    if g >= ngroups:
                        return
                    for j in range(2):
                        xt, xb, t0, h, hb = groups[g][j]
                        trh, trhb = trhr.next()
                        rms_part2(h, hb, idb, idbuf, trh, trhb, hTs[g % 2][:, :, j * 128:(j + 1) * 128], hTbss[g % 2][j])

                for i in range(4):
                    issue_load(i)
                norm1(0)
                norm2(0)
                for g in range(ngroups):
                    issue_load(2 * g + 4)
                    issue_load(2 * g + 5)
                    norm1(g + 1)
                    hT = hTs[g % 2]
                    hTbs = hTbss[g % 2]
                    xs = groups.pop(g)
                    for fc in range(32):
                        up, upb = upr.next()
                        for c in range(8):
                            k.op(PE, lambda e, c=c, fc=fc, up=up, hT=hT: e.matmul(
                                up[:, 0:256], wup[:, c, fc * 128:(fc + 1) * 128], hT[:, c, :], start=(c == 0), stop=(c == 7)),
                                reads=[wub, hTbs[0], hTbs[1]], writes=[upb], inc=(c == 7))
                        r, rb = rr.next()
                        k.op(ACT, lambda e, up=up, r=r: e.activation(out=r[:], in_=up[:, 0:256], func=AF.Relu),
                             reads=[upb], writes=[rb])
                        k.op(POOL, lambda e, r=r, fc=fc: e.tensor_tensor(out=aT[:, fc, :], in0=r[:], in1=r[:], op=ALU.mult),
                             reads=[rb], writes=[aTb])
                    norm2(g + 1)
                    for j in range(2):
                        xt, xb, t0 = xs[j][0:3]
                        xot, xob = xo.next()
                        for half in range(2):
                            dn, dnb = dnr.next()
                            for fc in range(32):
                                k.op(PE, lambda e, fc=fc, dn=dn, j=j, half=half: e.matmul(
                                    dn[:], aT[:, fc, j * 128:(j + 1) * 128], wdn[:, fc, half * 512:(half + 1) * 512],
                                    start=(fc == 0), stop=(fc == 31)),
                                    reads=[aTb, wdb], writes=[dnb], inc=(fc == 31))
                            k.op(DVE, lambda e, dn=dn, xt=xt, xot=xot, half=half: e.tensor_tensor(
                                out=xot[:, half * 512:(half + 1) * 512], in0=dn[:], in1=xt[:, half * 512:(half + 1) * 512], op=ALU.add),
                                reads=[dnb, xb], writes=[xob])
                        if layer == 1:
                            ss, ssb = ssr.next()
                            rs, rsb = rsr.next()
                            k.op(ACT, lambda e, xot=xot, ss=ss: e.activation(out=junk[:], in_=xot[:], func=AF.Square, scale=1.0 / 32, accum_out=ss[:]),
                                 reads=[xob], writes=[junkb, ssb])
                            k.op(ACT, lambda e, ss=ss, rs=rs: e.activation(out=rs[:], in_=ss[:], func=AF.Sqrt,
                                                                           bias=cur_eps["e6"][0][:], scale=1.0),
                                 reads=[ssb, cur_eps["e6"][1]], writes=[rsb])
                            k.op(DVE, lambda e, rs=rs: e.reciprocal(out=rs[:], in_=rs[:]), reads=[rsb], writes=[rsb])
                            k.op(DVE, lambda e, xot=xot, rs=rs: e.scalar_tensor_tensor(
                                out=xot[:], in0=xot[:], scalar=rs[:, 0:1], in1=gf[:], op0=ALU.mult, op1=ALU.mult),
                                reads=[xob, rsb, gfb], writes=[xob])
                        k.dma(SP, xdst[t0:t0 + 128, :], xot[:], dS[0], reads=[xob], store=True)
                        nstore[0] += 1
                k.end_phase()

        def run_pending(pend, force=False):
            nxt = []
            for c, f in pend:
                c -= 1
                if c <= 0 or force:
                    r = f()
                    if r is not None:
                        nxt.append(r)
                else:
                    nxt.append((c, f))
            pend[:] = nxt

        def drain(pend):
            while pend:
                run_pending(pend)

        def run_jobs(jobs, LAG=1, pend=None, flush=True):
            n = len(jobs)
            if pend is None:
                pend = []
            for i in range(n + LAG):
                if i < n:
                    jobs[i]["qk"]()
                    jobs[i]["exp"]()
                run_pending(pend)
                if i >= LAG:
                    j = jobs[i - LAG]
                    j["pv"]()
                    if "fin" in j:
                        r = j["fin"]()
                        if r is not None:
                            pend.append(r)
            if flush:
                drain(pend)

        def phase_attn0():
            k.begin_phase()
            with ExitStack() as st:
                dW, dC, dKa, dKb, dVb = k.dsem(), k.dsem(), k.dsem(), k.dsem(), k.dsem()
                dS = [k.dsem() for _ in range(2)]
                dQ = [k.dsem() for _ in range(2)]
                dVe = [k.dsem() for _ in range(2)]
                dVo = [k.dsem() for _ in range(2)]
                dX = [k.dsem() for _ in range(2)]
                Tmax = max(seq_lens)
                wout = sb(st, "wout", [128, 8, D], BF16); wob = Buf()
                dPC = k.dsem()

                def late_loads():
                    load_w_cast(st, wout, wob, w_out_e, 8, dW, after=osbbs[0])
                    precast([(w_up[0], wupb[0]), (w_down[0], wdnb[0]), (w_in_o, winob), (w_out_o, woutob)], dPC)
                idb, idbuf, idcast = load_ident(st, dC)
                ones = sb(st, "ones", [128, 64], BF16); onesb = Buf()
                k.op(DVE, lambda e: e.memset(ones[:], 1.0), writes=[onesb])
                PT = sb(st, "PT", [128, 8, 14 * 64], BF16); ptb = Buf()
                mk = sb(st, "mk", [128, 14 * 64], F32); mkb = Buf()
                k.dma(SP, mk[:], naMask.rearrange("p a j q -> p (a j q)"), dC, writes=[mkb])
                k.seal(dC, [mkb])
                idcast()
                gst = Ring([sb(st, f"gst{i}", [128, 14 * 64], F32) for i in range(2)])
                dG = [k.dsem() for _ in range(2)]
                for h in range(8):
                    g_, gb_ = gst.next()
                    k.dma(SP, g_[:], rpbG[:, h].rearrange("p a j q -> p (a j q)"), dG[h % 2], writes=[gb_])
                    k.op(DVE, lambda e, g_=g_, h=h: e.tensor_tensor(out=PT[:, h, :], in0=g_[:], in1=mk[:], op=ALU.add),
                         reads=[gb_, mkb], writes=[ptb])
                KTa = sb(st, "KTa", [128, 4, Tmax], BF16); ktab = Buf()
                KTb = sb(st, "KTb", [128, 2, Tmax], BF16); ktbb = Buf()
                Vb = sb(st, "Vb", [128, Tmax // 128, 128], BF16); vbb = Buf()
                Vwe = Ring([sb(st, f"Vwe{i}", [128, 8, 512], BF16) for i in range(2)])
                Vwo = Ring([sb(st, f"Vwo{i}", [128, 8, 512], BF16) for i in range(2)])
                QTr = Ring([sb(st, f"QT{i}", [128, 8, 512], BF16) for i in range(2)])
                Pr = Ring([sb(st, f"P{i}", [128, 2, 512], BF16) for i in range(3)])
                osbs = [sb(st, f"osb{i}", [128, 8, 512], BF16) for i in range(2)]
                osbbs = [Buf(), Buf()]
                rbcr = Ring([sb(st, f"rbc{i}", [128, 512], F32) for i in range(2)])
                xr = Ring([sb(st, f"x{i}", [128, D], F32) for i in range(2)])
                xo = Ring([sb(st, f"xo{i}", [128, D], F32) for i in range(2)])
                Sr = Ring([ps(st, f"S{i}", [128, 2, 512], F32) for i in range(2)])
                Or = Ring([ps(st, f"O{i}", [128, 512], F32) for i in range(2)])
                Ur = Ring([ps(st, f"U{i}", [128, 512], F32) for i in range(2)])
                qkv = qk0T.rearrange("(c p) t -> p c t", p=128)
                xcount = 0
                blocks = [(s, qb) for s, T in enumerate(seq_lens) for qb in range(T // 512)]
                loaded = {}

                def issue_block(bi):
                    if bi >= len(blocks):
                        return
                    s, qb = blocks[bi]
                    T = seq_lens[s]
                    base = seq_base[s]
                    R = T // 64
                    q0 = base + qb * 512
                    QT, QTb = QTr.next()
                    k.dma(SP, QT[:, 0:4, :], qkv[:, 0:4, q0:q0 + 512], dQ[bi % 2], writes=[QTb])
                    k.dma(SP, QT[:, 4:8, :], qkv[:, 8:12, q0:q0 + 512], dQ[bi % 2], writes=[QTb])
                    units = _na_units(qb * 8, R)
                    ks_e = sorted({u[0] for u in units if u[0] % 2 == 0})
                    ks_o = sorted({u[0] for u in units if u[0] % 2 == 1})
                    Ve, Veb = Vwe.next()
                    Vo, Vob = Vwo.next()
                    ne = (ks_e[-1] - ks_e[0]) // 2 + 1
                    no = (ks_o[-1] - ks_o[0]) // 2 + 1
                    assert ne <= 8 and no <= 8
                    te = base + ks_e[0] * 64
                    k.dma(SP, Ve[:, 0:ne, :], v0[te:te + ne * 128, 0:512].rearrange("(t p) f -> p t f", p=128),
                          dVe[bi % 2], writes=[Veb])
                    to = base + ks_o[0] * 64
                    k.dma(SP, Vo[:, 0:no, :], v0[to:to + no * 128, 0:512].rearrange("(t p) f -> p t f", p=128),
                          dVo[bi % 2], writes=[Vob])
                    loaded[bi] = (QT, QTb, units, ks_e, ks_o, Ve, Veb, Vo, Vob)

                pend = []
                prev_out = None
                xc = {"n": 0}

                def make_outproj(tt, q0, osb, osbb):
                    def f():
                        t0 = q0 + tt * 128
                        xt, xb = xr.next()
                        k.dma(SP, xt[:], xin[t0:t0 + 128, :], dX[xc["n"] % 2], writes=[xb])
                        xc["n"] += 1
                        xot, xob = xo.next()
                        S2, Sb_ = Sr.next()
                        for half in range(2):
                            for pr in range(8):
                                k.op(PE, lambda e, pr=pr, half=half: e.matmul(
                                    S2[:, half, :], osb[:, pr, tt * 128:(tt + 1) * 128], wout[:, pr, half * 512:(half + 1) * 512],
                                    start=(pr == 0), stop=(pr == 7)), reads=[osbb, wob], writes=[Sb_], inc=(pr == 7 and half == 1))
                        k.op(DVE, lambda e: e.tensor_tensor(
                            out=xot[:].rearrange("p (h f) -> p h f", h=2), in0=S2[:], in1=xt[:].rearrange("p (h f) -> p h f", h=2), op=ALU.add),
                            reads=[Sb_, xb], writes=[xob])
                        k.dma(SP, x1[t0:t0 + 128, :], xot[:], dS[xc["n"] % 2], reads=[xob], store=True)
                        return None
                    return f

                issue_block(0)
                for bi, (s, qb) in enumerate(blocks):
                    T = seq_lens[s]
                    base = seq_base[s]
                    if qb == 0:
                        k.dma(SP, KTa[:, :, 0:T], qkv[:, 4:8, base:base + T], dKa, writes=[ktab])
                        k.dma(SP, KTb[0:64, 0, 0:T], qk0T[12 * 128:12 * 128 + 64, base:base + T], dKb, writes=[ktbb])
                        k.dma(SP, KTb[64:128, 0, 0:T], qk0T[12 * 128:12 * 128 + 64, base:base + T], dKb, writes=[ktbb])
                        k.dma(SP, KTb[0:64, 1, 0:T], qk0T[12 * 128 + 64:13 * 128, base:base + T], dKb, writes=[ktbb])
                        k.dma(SP, KTb[64:128, 1, 0:T], qk0T[12 * 128 + 64:13 * 128, base:base + T], dKb, writes=[ktbb])
                        n4 = T // 512
                        for c4 in range(4):
                            k.dma(SP, Vb[:, c4 * n4:(c4 + 1) * n4, :],
                                  v0[base + c4 * n4 * 128:base + (c4 + 1) * n4 * 128, 512:640].rearrange("(t p) f -> p t f", p=128),
                                  dVb, writes=[vbb])
                    issue_block(bi + 1)
                    q0 = base + qb * 512
                    QT, QTb, units, ks_e, ks_o, Ve, Veb, Vo, Vob = loaded.pop(bi)
                    osb, osbb = osbs[bi % 2], osbbs[bi % 2]
                    jobs = []
                    for pr in range(8):
                        Oacc, Ob = Or.next()
                        Uacc, Ub = Ur.next()
                        first = {"v": True}
                        if pr < 4:
                            pc = pr
                            ha, hb = 2 * pr, 2 * pr + 1
                            for (ks, pos0, n, jpar, j20) in units:
                                nw = n * 64
                                c0 = pos0 * 64
                                jc0 = (jpar * 7 + j20) * 64
                                if ks % 2 == 0:
                                    Vt, Vtb, vi = Ve, Veb, (ks - ks_e[0]) // 2
                                else:
                                    Vt, Vtb, vi = Vo, Vob, (ks - ks_o[0]) // 2
                                cell = {}

                                def qk(cell=cell, pc=pc, ks=ks, c0=c0, nw=nw, ha=ha, hb=hb, jc0=jc0, QT=QT, QTb=QTb):
                                    S2, Sb_ = Sr.next()
                                    cell["S"] = (S2, Sb_)
                                    k.op(PE, lambda e: e.matmul(S2[:, 0, 0:nw], KTa[0:64, pc, ks * 64:ks * 64 + 128],
                                                                QT[0:64, pc, c0:c0 + nw], start=True, stop=False),
                                         reads=[ktab, QTb], writes=[Sb_], inc=False)
                                    k.op(PE, lambda e: e.matmul(S2[:, 1, 0:nw], KTa[64:128, pc, ks * 64:ks * 64 + 128],
                                                                QT[64:128, pc, c0:c0 + nw], start=True, stop=False),
                                         reads=[ktab, QTb], writes=[Sb_], inc=False)
                                    k.op(PE, lambda e: e.matmul(S2[:, 0, 0:nw], idb[:], PT[:, ha, jc0:jc0 + nw], start=False, stop=True),
                                         reads=[idbuf, ptb], writes=[Sb_], inc=False)
                                    k.op(PE, lambda e: e.matmul(S2[:, 1, 0:nw], idb[:], PT[:, hb, jc0:jc0 + nw], start=False, stop=True),
                                         reads=[idbuf, ptb], writes=[Sb_])

                                def ex(cell=cell, nw=nw):
                                    S2, Sb_ = cell["S"]
                                    P2, Pb_ = Pr.next()
                                    cell["P"] = (P2, Pb_)
                                    k.op(ACT, lambda e: e.activation(out=P2[:, :, 0:nw], in_=S2[:, :, 0:nw], func=AF.Exp),
                                         reads=[Sb_], writes=[Pb_])

                                def pv(cell=cell, nw=nw, c0=c0, Oacc=Oacc, Ob=Ob, Uacc=Uacc, Ub=Ub, Vt=Vt, Vtb=Vtb, vi=vi,
                                       ha=ha, hb=hb, first=first):
                                    P2, Pb_ = cell["P"]
                                    st_ = first["v"]
                                    first["v"] = False
                                    k.op(PE, lambda e: e.matmul(Oacc[0:64, c0:c0 + nw], Vt[:, vi, ha * 64:(ha + 1) * 64], P2[:, 0, 0:nw],
                                                                start=st_, stop=False, skip_group_check=True),
                                         reads=[Pb_, Vtb], writes=[Ob], inc=False)
                                    k.op(PE, lambda e: e.matmul(Oacc[64:128, c0:c0 + nw], Vt[:, vi, hb * 64:(hb + 1) * 64], P2[:, 1, 0:nw],
                                                                start=st_, stop=False, skip_group_check=True, tile_position=(0, 64)),
                                         reads=[Pb_, Vtb], writes=[Ob], inc=False)
                                    k.op(PE, lambda e: e.matmul(Uacc[0:64, c0:c0 + nw], ones[:, 0:64], P2[:, 0, 0:nw],
                                                                start=st_, stop=False, skip_group_check=True),
                                         reads=[Pb_, onesb], writes=[Ub], inc=False)
                                    k.op(PE, lambda e: e.matmul(Uacc[64:128, c0:c0 + nw], ones[:, 0:64], P2[:, 1, 0:nw],
                                                                start=st_, stop=False, skip_group_check=True, tile_position=(0, 64)),
                                         reads=[Pb_, onesb], writes=[Ub])
                                jobs.append({"qk": qk, "exp": ex, "pv": pv})
                        else:
                            gi = pr - 4
                            kv = gi // 2
                            pc = 4 + gi
                            nkt = T // 128
                            for kt in range(nkt):
                                cell = {}

                                def qk(cell=cell, pc=pc, kv=kv, kt=kt, QT=QT, QTb=QTb):
                                    S2, Sb_ = Sr.next()
                                    cell["S"] = (S2, Sb_)
                                    k.op(PE, lambda e: e.matmul(S2[:, 0, :], KTb[0:64, kv, kt * 128:(kt + 1) * 128], QT[0:64, pc, :],
                                                                start=True, stop=True), reads=[ktbb, QTb], writes=[Sb_], inc=False)
                                    k.op(PE, lambda e: e.matmul(S2[:, 1, :], KTb[64:128, kv, kt * 128:(kt + 1) * 128], QT[64:128, pc, :],
                                                                start=True, stop=True), reads=[ktbb, QTb], writes=[Sb_])

                                def ex(cell=cell):
                                    S2, Sb_ = cell["S"]
                                    P2, Pb_ = Pr.next()
                                    cell["P"] = (P2, Pb_)
                                    k.op(ACT, lambda e: e.activation(out=P2[:], in_=S2[:], func=AF.Exp), reads=[Sb_], writes=[Pb_])

                                def pv(cell=cell, Oacc=Oacc, Ob=Ob, Uacc=Uacc, Ub=Ub, kt=kt, kv=kv, nkt=nkt):
                                    P2, Pb_ = cell["P"]
                                    a, z = (kt == 0), (kt == nkt - 1)
                                    k.op(PE, lambda e: e.matmul(Oacc[0:64, :], Vb[:, kt, kv * 64:(kv + 1) * 64], P2[:, 0, :], start=a, stop=z),
                                         reads=[Pb_, vbb], writes=[Ob], inc=False)
                                    k.op(PE, lambda e: e.matmul(Oacc[64:128, :], Vb[:, kt, kv * 64:(kv + 1) * 64], P2[:, 1, :], start=a, stop=z,
                                                                tile_position=(0, 64)), reads=[Pb_, vbb], writes=[Ob], inc=False)
                                    k.op(PE, lambda e: e.matmul(Uacc[0:64, :], ones[:, 0:64], P2[:, 0, :], start=a, stop=z),
                                         reads=[Pb_, onesb], writes=[Ub], inc=False)
                                    k.op(PE, lambda e: e.matmul(Uacc[64:128, :], ones[:, 0:64], P2[:, 1, :], start=a, stop=z,
                                                                tile_position=(0, 64)), reads=[Pb_, onesb], writes=[Ub])
                                jobs.append({"qk": qk, "exp": ex, "pv": pv})

                        def fin(Oacc=Oacc, Ob=Ob, Uacc=Uacc, Ub=Ub, pr=pr, osb=osb, osbb=osbb):
                            rbc, rbcb = rbcr.next()
                            k.op(DVE, lambda e: e.reciprocal(out=rbc[:], in_=Uacc[:]), reads=[Ub], writes=[rbcb])
                            k.op(DVE, lambda e: e.tensor_tensor(
                                out=osb[:, pr, :].rearrange("p (r par c) -> p par r c", r=4, par=2),
                                in0=Oacc[:].rearrange("p (par r c) -> p par r c", par=2, r=4),
                                in1=rbc[:].rearrange("p (par r c) -> p par r c", par=2, r=4), op=ALU.mult),
                                reads=[Ob, rbcb], writes=[osbb])
                        jobs[-1]["fin"] = fin
                    if prev_out is not None:
                        for tt in range(4):
                            pend.append((6 + 2 * tt, make_outproj(tt, *prev_out)))
                    run_jobs(jobs, pend=pend, flush=False)
                    if bi == 0:
                        late_loads()
                    prev_out = (q0, osb, osbb)
                for tt in range(4):
                    pend.append((2 + 2 * tt, make_outproj(tt, *prev_out)))
                drain(pend)
                k.end_phase()

        def phase_attn1():
            k.begin_phase()
            with ExitStack() as st:
                dW, dC, dK, dV = k.dsem(), k.dsem(), k.dsem(), k.dsem()
                dS = [k.dsem() for _ in range(2)]
                dQ = [k.dsem() for _ in range(2)]
                dX = [k.dsem() for _ in range(2)]
                Tmax = max(seq_lens)
                wout = sb(st, "wout", [128, 8, D], BF16); wob = Buf()
                dPC = k.dsem()

                def late_loads():
                    load_w_bf16(wout, wob, woutob, 8, dW, step=4)
                    precast([(w_up[1], wupb[1]), (w_down[1], wdnb[1])], dPC)
                ones = sb(st, "ones", [128, 128], BF16); onesb = Buf()
                onesS = sb(st, "onesS", [128, 128], BF16)
                e5, e5b = mk_col(st, 1e-5)
                k.op(DVE, lambda e: e.memset(ones[:], 1.0), writes=[onesb])
                k.op(DVE, lambda e: e.memset(onesS[:], 1.0 / 128), writes=[onesb])
                lv = sb(st, "lv", [128, 4, 64], F32); lvb = Buf()
                for i in range(4):
                    k.dma(SP, lv[:, i, :], lamv[i, :].partition_broadcast(128), dC, writes=[lvb])
                gs = sb(st, "gs", [128, 1], F32); gsb = Buf()
                k.dma(SP, gs[:], subg[:, :], dC, writes=[gsb])
                k.seal(dC, [lvb, gsb])
                lp = sb(st, "lp", [128, 2, 64], F32); lpb = Buf()
                l2 = sb(st, "l2", [128, 2], F32); l2b = Buf()
                nlam = sb(st, "nlam", [128, 1], F32); nlamb = Buf()
                k.op(DVE, lambda e: e.tensor_tensor(out=lp[:, 0, :], in0=lv[:, 0, :], in1=lv[:, 1, :], op=ALU.mult), reads=[lvb], writes=[lpb])
                k.op(DVE, lambda e: e.tensor_tensor(out=lp[:, 1, :], in0=lv[:, 2, :], in1=lv[:, 3, :], op=ALU.mult), reads=[lvb], writes=[lpb])
                k.op(DVE, lambda e: e.tensor_reduce(out=l2[:], in_=lp[:], axis=AX.X, op=ALU.add), reads=[lpb], writes=[l2b])
                k.op(ACT, lambda e: e.activation(out=l2[:], in_=l2[:], func=AF.Exp), reads=[l2b], writes=[l2b])
                k.op(DVE, lambda e: e.tensor_tensor(out=nlam[:], in0=l2[:, 1:2], in1=l2[:, 0:1], op=ALU.subtract), reads=[l2b], writes=[nlamb])
                k.op(DVE, lambda e: e.tensor_scalar(out=nlam[:], in0=nlam[:], scalar1=-LAM_INIT1, scalar2=1.0, op0=ALU.add, op1=ALU.mult),
                     reads=[nlamb], writes=[nlamb])
                k.op(DVE, lambda e: e.tensor_scalar(out=gs[:], in0=gs[:], scalar1=(1.0 - LAM_INIT1), scalar2=0.0, op0=ALU.mult, op1=ALU.add),
                     reads=[gsb], writes=[gsb])
                KT = sb(st, "KT", [128, 8, Tmax], BF16); ktb = Buf()
                V = sb(st, "V", [128, Tmax // 128, D], BF16); vb = Buf()
                QTr = Ring([sb(st, f"QT{i}", [128, 8, 512], BF16) for i in range(2)])
                Pr = Ring([sb(st, f"P{i}", [128, 2, 512], BF16) for i in range(3)])
                osbs = [sb(st, f"osb{i}", [128, 8, 512], BF16) for i in range(2)]
                osbbs = [Buf(), Buf()]
                o1 = sb(st, "o1", [128, 512], F32); o1b = Buf()
                o2 = sb(st, "o2", [128, 512], F32); o2b = Buf()
                us = sb(st, "us", [128, 512], F32); usb = Buf()
                sel0 = sb(st, "sel0", [128, 128], F32)
                sel1 = sb(st, "sel1", [128, 128], F32)
                selb = Buf()
                k.op(DVE, lambda e: e.memset(sel0[0:64, :], 1.0 / 64), writes=[selb])
                k.op(DVE, lambda e: e.memset(sel0[64:128, :], 0.0), writes=[selb])
                k.op(DVE, lambda e: e.memset(sel1[0:64, :], 0.0), writes=[selb])
                k.op(DVE, lambda e: e.memset(sel1[64:128, :], 1.0 / 64), writes=[selb])
                sqt = sb(st, "sqt", [128, 512], BF16); sqb = Buf()
                rst = sb(st, "rst", [128, 512], F32); rstb = Buf()
                xr = Ring([sb(st, f"x{i}", [128, D], F32) for i in range(1)])
                xo = Ring([sb(st, f"xo{i}", [128, D], F32) for i in range(2)])
                Sr = Ring([ps(st, f"S{i}", [128, 2, 512], F32) for i in range(2)])
                O1 = ps(st, "O1", [128, 512], F32); O1b = Buf()
                O2 = ps(st, "O2", [128, 512], F32); O2b = Buf()
                U = ps(st, "U", [128, 512], F32); Ub = Buf()
                qkv = qk1T.rearrange("(c p) t -> p c t", p=128)
                xcount = 0
                blocks = [(s, qb) for s, T in enumerate(seq_lens) for qb in range(T // 512)]
                loaded = {}

                def issue_block(bi):
                    if bi >= len(blocks):
                        return
                    s, qb = blocks[bi]
                    q0 = seq_base[s] + qb * 512
                    QT, QTb = QTr.next()
                    k.dma(SP, QT[:], qkv[:, 0:8, q0:q0 + 512], dQ[bi % 2], writes=[QTb])
                    loaded[bi] = (QT, QTb)

                pend = []
                prev_out = None
                xc = {"n": 0}

                def make_outproj(tt, q0, osb, osbb):
                    def f():
                        t0 = q0 + tt * 128
                        xt, xb = xr.next()
                        k.dma(SP, xt[:], x2[t0:t0 + 128, :], dX[xc["n"] % 2], writes=[xb])
                        xc["n"] += 1
                        xot, xob = xo.next()
                        S2, Sb_ = Sr.next()
                        for half in range(2):
                            for h in range(8):
                                k.op(PE, lambda e, h=h, half=half: e.matmul(
                                    S2[:, half, :], osb[:, h, tt * 128:(tt + 1) * 128], wout[:, h, half * 512:(half + 1) * 512],
                                    start=(h == 0), stop=(h == 7)), reads=[osbb, wob], writes=[Sb_], inc=(h == 7 and half == 1))
                        k.op(DVE, lambda e: e.tensor_tensor(
                            out=xot[:].rearrange("p (h f) -> p h f", h=2), in0=S2[:], in1=xt[:].rearrange("p (h f) -> p h f", h=2), op=ALU.add),
                            reads=[Sb_, xb], writes=[xob])
                        k.dma(SP, x3[t0:t0 + 128, :], xot[:], dS[xc["n"] % 2], reads=[xob], store=True)
                        return None
                    return f

                issue_block(0)
                for bi, (s, qb) in enumerate(blocks):
                    T = seq_lens[s]
                    base = seq_base[s]
                    nkt = T // 128
                    if qb == 0:
                        for c in range(8):
                            k.dma(SP, KT[:, c, 0:T], qkv[:, 8 + c, base:base + T], dK, writes=[ktb])
                        for c in range(4):
                            n4 = nkt // 4
                            k.dma(SP, V[:, c * n4:(c + 1) * n4, :],
                                  v1[base + c * n4 * 128:base + (c + 1) * n4 * 128, :].rearrange("(t p) f -> p t f", p=128), dV, writes=[vb])
                    issue_block(bi + 1)
                    q0 = base + qb * 512
                    QT, QTb = loaded.pop(bi)
                    osb, osbb = osbs[bi % 2], osbbs[bi % 2]
                    jobs = []
                    for h in range(8):
                        for kt in range(nkt):
                            cell = {}

                            def qk(cell=cell, h=h, kt=kt, QT=QT, QTb=QTb):
                                S2, Sb_ = Sr.next()
                                cell["S"] = (S2, Sb_)
                                k.op(PE, lambda e: e.matmul(S2[:, 0, :], KT[0:64, h, kt * 128:(kt + 1) * 128], QT[0:64, h, :],
                                                            start=True, stop=True), reads=[ktb, QTb], writes=[Sb_], inc=False)
                                k.op(PE, lambda e: e.matmul(S2[:, 1, :], KT[64:128, h, kt * 128:(kt + 1) * 128], QT[64:128, h, :],
                                                            start=True, stop=True), reads=[ktb, QTb], writes=[Sb_])

                            def ex(cell=cell):
                                S2, Sb_ = cell["S"]
                                P2, Pb_ = Pr.next()
                                cell["P"] = (P2, Pb_)
                                k.op(ACT, lambda e: e.activation(out=P2[:], in_=S2[:], func=AF.Exp), reads=[Sb_], writes=[Pb_])

                            def pv(cell=cell, kt=kt, h=h, nkt=nkt):
                                P2, Pb_ = cell["P"]
                                a, z = (kt == 0), (kt == nkt - 1)
                                k.op(PE, lambda e: e.matmul(O1[:], V[:, kt, h * 128:(h + 1) * 128], P2[:, 0, :], start=a, stop=z),
                                     reads=[Pb_, vb], writes=[O1b], inc=False)
                                k.op(PE, lambda e: e.matmul(O2[:], V[:, kt, h * 128:(h + 1) * 128], P2[:, 1, :], start=a, stop=z),
                                     reads=[Pb_, vb], writes=[O2b], inc=False)
                                k.op(PE, lambda e: e.matmul(U[0:64, :], ones[:, 0:64], P2[:, 0, :], start=a, stop=z),
                                     reads=[Pb_, onesb], writes=[Ub], inc=False)
                                k.op(PE, lambda e: e.matmul(U[64:128, :], ones[:, 0:64], P2[:, 1, :], start=a, stop=z,
                                                            tile_position=(0, 64)),
                                     reads=[Pb_, onesb], writes=[Ub])
                            jobs.append({"qk": qk, "exp": ex, "pv": pv})

                        def fin(h=h, osb=osb, osbb=osbb):
                            k.op(DVE, lambda e: e.tensor_copy(out=o1[:], in_=O1[:]), reads=[O1b], writes=[o1b])
                            k.op(DVE, lambda e: e.tensor_copy(out=o2[:], in_=O2[:]), reads=[O2b], writes=[o2b])
                            k.op(DVE, lambda e: e.tensor_copy(out=us[:], in_=U[:]), reads=[Ub], writes=[usb])

                            def fin2(h=h):
                                k.op(DVE, lambda e: e.reciprocal(out=us[:], in_=us[:]), reads=[usb], writes=[usb])

                                def fin2b(h=h):
                                    S2, Sb_ = Sr.next()
                                    k.op(PE, lambda e: e.matmul(S2[:, 0, :], sel0[:], us[:], start=True, stop=True),
                                         reads=[usb, selb], writes=[Sb_], inc=False)
                                    k.op(PE, lambda e: e.matmul(S2[:, 1, :], sel1[:], us[:], start=True, stop=True),
                                         reads=[usb, selb], writes=[Sb_])
                                    k.op(DVE, lambda e: e.tensor_tensor(out=o1[:], in0=o1[:], in1=S2[:, 0, :], op=ALU.mult),
                                         reads=[o1b, Sb_], writes=[o1b])
                                    k.op(DVE, lambda e: e.tensor_tensor(out=o2[:], in0=o2[:], in1=S2[:, 1, :], op=ALU.mult),
                                         reads=[o2b, Sb_], writes=[o2b])
                                    k.op(DVE, lambda e: e.scalar_tensor_tensor(out=o1[:], in0=o2[:], scalar=nlam[:, 0:1], in1=o1[:],
                                                                               op0=ALU.mult, op1=ALU.add),
                                         reads=[o2b, nlamb, o1b], writes=[o1b])
                                    k.op(POOL, lambda e: e.tensor_tensor(out=sqt[:], in0=o1[:], in1=o1[:], op=ALU.mult),
                                         reads=[o1b], writes=[sqb])
                                    return (4, fin3)

                                def fin3(h=h):
                                    S2, Sb_ = Sr.next()
                                    k.op(PE, lambda e: e.matmul(S2[:, 0, :], onesS[:], sqt[:], start=True, stop=True),
                                         reads=[sqb, onesb], writes=[Sb_])
                                    k.op(ACT, lambda e: e.activation(out=rst[:], in_=S2[:, 0, :], func=AF.Ln, bias=e5[:], scale=1.0),
                                         reads=[Sb_, e5b], writes=[rstb])
                                    k.op(ACT, lambda e: e.activation(out=rst[:], in_=rst[:], func=AF.Exp, scale=-0.5),
                                         reads=[rstb], writes=[rstb])
                                    k.op(DVE, lambda e: e.scalar_tensor_tensor(out=osb[:, h, :], in0=o1[:], scalar=gs[:, 0:1], in1=rst[:],
                                                                               op0=ALU.mult, op1=ALU.mult),
                                         reads=[o1b, gsb, rstb], writes=[osbb])
                                return (4, fin2b)
                            return (1, fin2)
                        jobs[-1]["fin"] = fin
                    if prev_out is not None:
                        for tt in range(4):
                            pend.append((12 + 2 * tt, make_outproj(tt, *prev_out)))
                    run_jobs(jobs, pend=pend, flush=False)
                    if bi == 0:
                        late_loads()
                    prev_out = (q0, osb, osbb)
                for tt in range(4):
                    pend.append((12 + 2 * tt, make_outproj(tt, *prev_out)))
                drain(pend)
                k.end_phase()

        phases = [lambda: phase_inproj(0), phase_attn0, lambda: phase_mlp(0),
                  lambda: phase_inproj(1), phase_attn1, lambda: phase_mlp(1)]
        for p in phases[:nphases]:
            p()
    return nc


_ROPE = None


def _consts(rpb):
    global _ROPE
    if _ROPE is None:
        _ROPE = _rope_tables()
    g, m = _na_consts(np.asarray(rpb, dtype=np.float32)[0])
    return {"ropeT": _ROPE, "rpbG": g, "naMask": m, "identF": np.eye(128, dtype=np.float32)}


def make_in_maps(x_list, p):
    f = lambda a: np.ascontiguousarray(np.asarray(a, dtype=np.float32))
    c = _consts(p["rpb"])
    shared = {
        "w_in_e": f(p["w_in_e"][0]), "w_out_e": f(p["w_out_e"][0]),
        "w_in_o": f(p["w_in_o"][0]), "w_out_o": f(p["w_out_o"][0]),
        "w_up": f(p["w_up"]), "w_down": f(p["w_down"]),
        "gains": f(np.stack([p["ln_mix_e"][0], p["ln_mlp"][0], p["ln_mix_o"][0], p["ln_mlp"][1], p["ln_f"]])),
        "qkn": f(np.stack([p["q_norm_b"][0], p["k_norm_b"][0]])),
        "lamv": f(np.stack([p["lambda_q1"][0], p["lambda_k1"][0], p["lambda_q2"][0], p["lambda_k2"][0]])),
        "subg": f(np.asarray(p["subln_g"][0]).reshape(128, 1)),
        **c,
    }
    return [dict(shared, xin=f(x)) for x in x_list]


def kernel(x_prompt, x_sample, **p):
    x_prompt = np.asarray(x_prompt, dtype=np.float32)
    x_sample = np.asarray(x_sample, dtype=np.float32)
    n = 8
    seq_lens = [2048, 4096, 4096]
    xs = []
    for c in range(n):
        xs.append(np.concatenate([x_prompt[c], x_sample[2 * c], x_sample[2 * c + 1]], axis=0))
    nc = build(seq_lens)
    in_maps = make_in_maps(xs, p)
    res = run_bass_kernel_spmd(nc, in_maps, core_ids=list(range(n)))
    yp = np.empty((8, 2048, D), np.float32)
    ysm = np.empty((16, 4096, D), np.float32)
    for c in range(n):
        yy = res.results[c]["y"]
        yp[c] = yy[0:2048]
        ysm[2 * c] = yy[2048:6144]
        ysm[2 * c + 1] = yy[6144:10240]
    return (yp, ysm)
```

```python
import math
from contextlib import ExitStack
import numpy as np
import ml_dtypes
import concourse.bass as bass
import concourse.mybir as mybir
from concourse.bass_utils import run_bass_kernel_spmd

F32 = mybir.dt.float32
BF16 = mybir.dt.bfloat16
AF = mybir.ActivationFunctionType
ALU = mybir.AluOpType
AX = mybir.AxisListType

D = 1024
NEG = -30000.0
LAM_INIT1 = 0.8 - 0.6 * math.exp(-0.3 * 1)
SAME_ENGINE_SYNC = True


class Buf:
    __slots__ = ("w", "r", "name")

    def __init__(self, name=""):
        self.w = None
        self.r = {}
        self.name = name


class Eng:
    def __init__(self, name, attr):
        self.name = name
        self.attr = attr
        self.items = []
        self.seen = {}
        self.sem = None
        self.cnt = 0


class DSem:
    def __init__(self, sem):
        self.sem = sem
        self.cnt = 0


class K:
    def __init__(self, nc, sems):
        self.nc = nc
        self.sems = sems
        self.sem_i = 0
        self.PE = Eng("pe", "tensor")
        self.ACT = Eng("act", "scalar")
        self.DVE = Eng("dve", "vector")
        self.POOL = Eng("pool", "gpsimd")
        self.SP = Eng("sp", "sync")
        self.engs = [self.PE, self.ACT, self.DVE, self.POOL, self.SP]
        self.store_toks = []

    def new_sem(self):
        s = self.sems[self.sem_i]
        self.sem_i += 1
        return s

    def begin_phase(self):
        for e in self.engs:
            e.items = []
            e.seen = {}
            e.sem = self.new_sem() if e.name != "sp" else None
            e.cnt = 0
        self.store_toks = []

    def dsem(self):
        return DSem(self.new_sem())

    def _deps(self, E, reads, writes):
        waits = []

        def need(s, v, raw):
            if s is E.sem:
                if E.name == "pe" or not SAME_ENGINE_SYNC or not raw:
                    return
            if E.seen.get(id(s), 0) >= v:
                return
            E.seen[id(s)] = v
            waits.append((s, v))

        for b in reads:
            if b.w is not None:
                need(*b.w, True)
        for b in writes:
            if b.w is not None:
                need(*b.w, False)
            for s, v in b.r.values():
                need(s, v, False)
        return waits

    def _commit(self, tok, reads, writes):
        for b in reads:
            b.r[id(tok[0])] = tok
        for b in writes:
            b.w = tok
            b.r = {}

    def op(self, E, fn, reads=(), writes=(), inc=True):
        waits = self._deps(E, reads, writes)
        if inc:
            E.cnt += 1
            tok = (E.sem, E.cnt)
        else:
            tok = (E.sem, E.cnt + 1)
        sem = E.sem

        def run(eng, fn=fn, waits=waits, inc=inc, sem=sem):
            for s, v in waits:
                eng.wait_ge(s, v)
            ins = fn(eng)
            if inc:
                ins.then_inc(sem, 1)

        E.items.append(run)
        self._commit(tok, reads, writes)
        return tok

    def dma(self, E, out, in_, ds, reads=(), writes=(), store=False):
        waits = self._deps(E, reads, writes)
        ds.cnt += 1
        tok = (ds.sem, ds.cnt * 16)

        def run(eng, waits=waits, out=out, in_=in_, sem=ds.sem):
            for s, v in waits:
                eng.wait_ge(s, v)
            eng.dma_start(out=out, in_=in_).then_inc(sem, 16)

        E.items.append(run)
        self._commit(tok, reads, writes)
        if store:
            self.store_toks.append(tok)
        return tok

    def seal(self, ds, bufs):
        for b in bufs:
            b.w = (ds.sem, ds.cnt * 16)

    def end_phase(self):
        final = {}
        for s, v in self.store_toks:
            if final.get(id(s), (s, 0))[1] < v:
                final[id(s)] = (s, v)
        fin = list(final.values())

        def run(eng, fin=fin):
            for s, v in fin:
                eng.wait_ge(s, v)

        self.SP.items.append(run)
        nc = self.nc
        with nc.Block() as block:
            @block.tensor
            def _(eng):
                for it in self.PE.items:
                    it(eng)

            @block.scalar
            def _(eng):
                for it in self.ACT.items:
                    it(eng)

            @block.vector
            def _(eng):
                for it in self.DVE.items:
                    it(eng)

            @block.gpsimd
            def _(eng):
                for it in self.POOL.items:
                    it(eng)

            @block.sync
            def _(eng):
                for it in self.SP.items:
                    it(eng)


class Ring:
    def __init__(self, aps, name="ring"):
        self.aps = aps
        self.bufs = [Buf(f"{name}{i}") for i in range(len(aps))]
        self.i = 0

    def next(self):
        j = self.i % len(self.aps)
        self.i += 1
        return self.aps[j], self.bufs[j]


def _rope_tables():
    f32 = np.float32

    def angles(pos, dim, theta=10000.0):
        inv = (1.0 / np.power(f32(theta), np.arange(0, dim, 2, dtype=f32) / f32(dim))).astype(f32)
        ang = pos.astype(f32)[:, None] * inv[None, :]
        return np.cos(ang).astype(f32), np.sin(ang).astype(f32)

    t = np.arange(4096)
    cr, sr = angles(t // 64, 32)
    cc, sc = angles(t % 64, 32)
    cosA = np.concatenate([cr, cr, cc, cc], axis=1)
    sinA = np.concatenate([-sr, sr, -sc, sc], axis=1)
    c1, s1 = angles(t, 64)
    cos1 = np.concatenate([c1, c1], axis=1)
    sin1 = np.concatenate([-s1, s1], axis=1)
    tabs = np.stack([cosA * 0.125, sinA * 0.125, cosA, sinA, cos1 * 0.125, sin1 * 0.125, cos1, sin1]).astype(f32)
    tabs = tabs.reshape(8, 32, 128, 64).transpose(0, 2, 1, 3).reshape(8, 128, 32 * 64)
    return np.ascontiguousarray(tabs)


def _na_consts(rpb):
    a = np.arange(2)[:, None, None, None]
    kc = np.arange(64)[None, :, None, None]
    j = np.arange(14)[None, None, :, None]
    qc = np.arange(64)[None, None, None, :]
    drow = (6 - j) + a + 7 + 0 * kc + 0 * qc
    dcol = np.clip(kc - qc + 15, 0, 30) + 0 * a + 0 * j
    g = rpb[:, drow, dcol]
    g = g.transpose(1, 2, 0, 3, 4).reshape(128, 8, 7, 2, 64).transpose(0, 1, 3, 2, 4)
    g = np.ascontiguousarray(g).astype(np.float32)
    win0 = np.clip(qc - 8, 0, 48)
    ok = (kc >= win0) & (kc < win0 + 16)
    m = np.where(ok, 0.0, NEG).astype(np.float32) + 0 * a + 0 * j
    m = np.broadcast_to(m, (2, 64, 14, 64)).reshape(128, 7, 2, 64).transpose(0, 2, 1, 3)
    m = np.ascontiguousarray(m).astype(np.float32)
    return g, m


def _na_units(qr0, R):
    by = {}
    for i in range(8):
        qr = qr0 + i
        r0 = min(max(qr - 4, 0), R - 8)
        for m in range(4):
            by.setdefault((r0 + 2 * m, i % 2), []).append(i // 2)
    units = []
    for (ks, par) in sorted(by):
        lst = sorted(by[(ks, par)])
        while lst:
            n = 1
            while n < len(lst) and lst[n] == lst[n - 1] + 1:
                n += 1
            r2 = lst[0]
            j = 6 - ks + qr0 + 2 * r2 + par
            assert 0 <= j and j + 2 * (n - 1) <= 13
            units.append((ks, par * 4 + r2, n, j % 2, j // 2))
            lst = lst[n:]
    return units


def build(seq_lens, nphases=6, debug=False):
    Ttot = sum(seq_lens)
    nc = bass.Bass("TRN2", target_bir_lowering=False)

    def din(name, shape, dt=F32):
        return nc.dram_tensor(name, list(shape), dt, kind="ExternalInput").ap()

    def dscr(name, shape, dt):
        return nc.dram_tensor(name, list(shape), dt, kind=("ExternalOutput" if debug else "Internal")).ap()

    xin = din("xin", [Ttot, D])
    w_in_e = din("w_in_e", [D, 2304])
    w_out_e = din("w_out_e", [D, D])
    w_in_o = din("w_in_o", [D, 3072])
    w_out_o = din("w_out_o", [D, D])
    w_up = din("w_up", [2, D, 4096])
    w_down = din("w_down", [2, 4096, D])
    gains = din("gains", [5, D])
    qkn = din("qkn", [2, 64])
    lamv = din("lamv", [4, 64])
    subg = din("subg", [128, 1])
    rpbG = din("rpbG", [128, 8, 2, 7, 64])
    naMask = din("naMask", [128, 2, 7, 64])
    ropeT = din("ropeT", [8, 128, 32 * 64])
    identF = din("identF", [128, 128])
    y = nc.dram_tensor("y", [Ttot, D], F32, kind="ExternalOutput").ap()

    qk0T = dscr("qk0T", [13 * 128, Ttot], BF16)
    v0 = dscr("v0", [Ttot, 640], BF16)
    x1 = dscr("x1", [Ttot, D], F32)
    x2 = dscr("x2", [Ttot, D], F32)
    qk1T = dscr("qk1T", [16 * 128, Ttot], BF16)
    v1 = dscr("v1", [Ttot, D], BF16)
    x3 = dscr("x3", [Ttot, D], F32)
    wupb = nc.dram_tensor("wupb", [2, D, 4096], BF16, kind="Internal").ap()
    wdnb = nc.dram_tensor("wdnb", [2, 4096, D], BF16, kind="Internal").ap()
    winob = nc.dram_tensor("winob", [D, 3072], BF16, kind="Internal").ap()
    woutob = nc.dram_tensor("woutob", [D, D], BF16, kind="Internal").ap()

    seq_base = [sum(seq_lens[:i]) for i in range(len(seq_lens))]
    tiles128 = []
    for s, T in enumerate(seq_lens):
        for tt in range(T // 128):
            tiles128.append((seq_base[s] + tt * 128, tt))

    with ExitStack() as gstack:
        sems = [gstack.enter_context(nc.semaphore(f"s{i}")) for i in range(100)]
        k = K(nc, sems)
        PE, ACT, DVE, POOL, SP = k.PE, k.ACT, k.DVE, k.POOL, k.SP

        uid = [0]

        def sb(st, name, shape, dt):
            uid[0] += 1
            return st.enter_context(nc.sbuf_tensor(f"sb{uid[0]}_{name}", list(shape), dt))

        def ps(st, name, shape, dt):
            uid[0] += 1
            return st.enter_context(nc.psum_tensor(f"ps{uid[0]}_{name}", list(shape), dt))

        def load_ident(st, dsm):
            idf = sb(st, "idf", [128, 128], F32)
            idb = sb(st, "idb", [128, 128], BF16)
            b1, b2 = Buf(), Buf()
            k.dma(SP, idf[:], identF[:, :], dsm, writes=[b1])

            def cast():
                k.seal(dsm, [b1])
                k.op(DVE, lambda e: e.tensor_copy(out=idb[:], in_=idf[:]), reads=[b1], writes=[b2])
            return idb, b2, cast

        def mk_col(st, val):
            t = sb(st, "col", [128, 1], F32)
            b = Buf()
            k.op(DVE, lambda e: e.memset(t[:], val), writes=[b])
            return t, b

        def load_bc(st, name, src_row, n, dsm):
            t = sb(st, name, [128, n], F32)
            b = Buf(name)
            k.dma(SP, t[:], src_row.partition_broadcast(128), dsm, writes=[b])
            return t, b

        def pool_join(st, bufs, wbuf):
            jt = sb(st, "join", [128, 1], F32)
            k.op(POOL, lambda e: e.memset(jt[:], 0.0), reads=bufs, writes=[wbuf])

        def load_w_cast(st, wsb, wbuf, src, nchunks, dsm, after=None):
            v = src.rearrange("(c p) f -> p c f", p=128)
            ds2 = [dsm, k.dsem()]
            thr = [Buf(), Buf()]
            for c in range(nchunks):
                k.dma(POOL, wsb[:, c, :], v[:, c, :], ds2[c % 2], writes=[thr[c % 2]],
                      reads=([after] if (after is not None and c == 0) else []))
            pool_join(st, thr, wbuf)

        def precast(pairs, dsm, after=None):
            ds2 = [dsm, k.dsem()]
            thr = [Buf(), Buf()]
            j = 0
            for src, dst in pairs:
                vs = src.rearrange("(c p) f -> p c f", p=128)
                vd_ = dst.rearrange("(c p) f -> p c f", p=128)
                nchunk = vs.shape[1]
                step = max(1, 4096 // vs.shape[2])
                for c in range(0, nchunk, step):
                    k.dma(POOL, vd_[:, c:c + step, :], vs[:, c:c + step, :], ds2[j % 2], writes=[thr[j % 2]], store=True,
                          reads=([after] if (after is not None and j == 0) else []))
                    j += 1

        def load_w_bf16(wsb, wbuf, src, nchunks, dsm, step=1):
            v = src.rearrange("(c p) f -> p c f", p=128)
            for c in range(0, nchunks, step):
                k.dma(SP, wsb[:, c:c + step, :], v[:, c:c + step, :], dsm, writes=[wbuf])

        cur_eps = {}

        def rms_part1(xt, xb, g_bc, gb, junk, junkb, ssr, rsr, h_ring):
            ss, ssb = ssr.next()
            rs, rsb = rsr.next()
            h, hb = h_ring.next()
            k.op(ACT, lambda e: e.activation(out=junk[:], in_=xt, func=AF.Square, scale=1.0 / 32, accum_out=ss[:]),
                 reads=[xb], writes=[junkb, ssb])
            ep, epb = cur_eps["e6"]
            k.op(ACT, lambda e: e.activation(out=rs[:], in_=ss[:], func=AF.Sqrt, bias=ep[:], scale=1.0),
                 reads=[ssb, epb], writes=[rsb])
            k.op(DVE, lambda e: e.reciprocal(out=rs[:], in_=rs[:]), reads=[rsb], writes=[rsb])
            k.op(DVE, lambda e: e.scalar_tensor_tensor(out=h[:], in0=xt, scalar=rs[:, 0:1], in1=g_bc,
                                                       op0=ALU.mult, op1=ALU.mult),
                 reads=[xb, rsb, gb], writes=[hb])
            return h, hb

        def rms_part2(h, hb, idb, idbuf, trp, trpb, hT_ap, hTb):
            for c in range(8):
                k.op(PE, lambda e, c=c: e.transpose(out=trp[:, c * 128:(c + 1) * 128], in_=h[:, c * 128:(c + 1) * 128],
                                                    identity=idb[:]),
                     reads=[hb, idbuf], writes=[trpb], inc=(c == 7))
            k.op(ACT, lambda e: e.copy(out=hT_ap, in_=trp[:].rearrange("p (c t) -> p c t", c=8)),
                 reads=[trpb], writes=[hTb])

        def rope_evac_gen(src, srcb, width, nH, S, hs, cosT, sinT, tabb, tmp, tmpb, u, ub, dst, dstb, src_is_psum=True):
            def v5(ap):
                return ap.rearrange("p (h s t i) -> p h s t i", h=nH, s=S, t=2, i=hs)

            def tb(ap):
                return ap.unsqueeze(1).to_broadcast([128, nH, 64])

            def tb5(ap, half):
                a5 = ap.rearrange("p (s t i) -> p s t i", s=S, t=2, i=hs)[:, :, half, :]
                return a5.unsqueeze(1).to_broadcast([128, nH, S, hs])

            e1 = DVE
            k.op(e1, lambda e: e.tensor_tensor(out=tmp[:, 0:width].rearrange("p (h d) -> p h d", h=nH),
                                               in0=src.rearrange("p (h d) -> p h d", h=nH),
                                               in1=tb(cosT), op=ALU.mult),
                 reads=[srcb, tabb], writes=[tmpb])
            yield
            k.op(e1, lambda e: e.tensor_tensor(out=v5(u[:, 0:width])[:, :, :, 0, :], in0=v5(src)[:, :, :, 1, :],
                                               in1=tb5(sinT, 0), op=ALU.mult),
                 reads=[srcb, tabb], writes=[ub])
            yield
            k.op(e1, lambda e: e.tensor_tensor(out=v5(u[:, 0:width])[:, :, :, 1, :], in0=v5(src)[:, :, :, 0, :],
                                               in1=tb5(sinT, 1), op=ALU.mult),
                 reads=[srcb, tabb], writes=[ub])
            yield
            k.op(POOL, lambda e: e.tensor_tensor(out=dst, in0=tmp[:, 0:width], in1=u[:, 0:width], op=ALU.add),
                 reads=[tmpb, ub], writes=[dstb])
            yield

        def rope_evac(*a, **kw):
            for _ in rope_evac_gen(*a, **kw):
                pass

        def run_interleaved(gens):
            gens = list(gens)
            while gens:
                for g in list(gens):
                    try:
                        next(g)
                    except StopIteration:
                        gens.remove(g)

        def phase_inproj(layer):
            k.begin_phase()
            if layer == 0:
                w_src, F, gi = w_in_e, 2304, 0
                xsrc, qkT_d, v_d, nchunk, vw = xin, qk0T, v0, 13, 640
                tab0 = 0
            else:
                w_src, F, gi = w_in_o, 3072, 2
                xsrc, qkT_d, v_d, nchunk, vw = x2, qk1T, v1, 16, 1024
                tab0 = 4
            qkw = nchunk * 128
            with ExitStack() as st:
                cur_eps["e6"] = mk_col(st, 1e-6)
                dW, dC = k.dsem(), k.dsem()
                dS1 = [k.dsem() for _ in range(2)]
                dS2 = [k.dsem() for _ in range(2)]
                dXs = [k.dsem() for _ in range(3)]
                wsb = sb(st, "wsb", [128, 8, F], BF16)
                wb = Buf("w")
                if layer == 0:
                    load_w_cast(st, wsb, wb, w_src, 8, dW)
                else:
                    load_w_bf16(wsb, wb, winob, 8, dW)
                idb, idbuf, idcast = load_ident(st, dC)
                g_bc, gb = load_bc(st, "g_bc", gains[gi, :], D, dC)
                TT = 32
                tabs = sb(st, "tabs", [128, 4, TT, 64], F32)
                tabb = Buf("tabs")
                for i in range(4):
                    k.dma(SP, tabs[:, i, :, :].rearrange("p t d -> p (t d)"), ropeT[tab0 + i], dC, writes=[tabb])
                cb = [gb, tabb]
                if layer == 0:
                    gq, gqb = load_bc(st, "gq", qkn[0, :], 64, dC)
                    gk, gkb = load_bc(st, "gk", qkn[1, :], 64, dC)
                    cb += [gqb, gkb]
                k.seal(dC, cb)
                idcast()
                xr = Ring([sb(st, f"x{i}", [128, D], F32) for i in range(3)], "x")
                junk = sb(st, "junk", [128, D], BF16)
                junkb = Buf()
                ssr = Ring([sb(st, f"ss{i}", [128, 1], F32) for i in range(2)])
                rsr = Ring([sb(st, f"rs{i}", [128, 1], F32) for i in range(2)])
                hr = Ring([sb(st, f"h{i}", [128, D], BF16) for i in range(2)])
                hTr = Ring([sb(st, f"hT{i}", [128, 8, 128], BF16) for i in range(2)])
                qkr = Ring([sb(st, f"qktm{i}", [128, qkw], BF16) for i in range(2)])
                vst = Ring([sb(st, f"vst{i}", [128, 4, vw], BF16) for i in range(2)])
                qst = Ring([sb(st, f"qst{i}", [128, nchunk, 512], BF16) for i in range(2)])
                tmpr = Ring([sb(st, f"tmp{i}", [128, 512], F32) for i in range(2)])
                ur = Ring([sb(st, f"u{i}", [128, 512], F32) for i in range(2)])
                if layer == 0:
                    sqr = Ring([sb(st, f"sq{i}", [128, 512], F32) for i in range(2)])
                    xnr = Ring([sb(st, f"xn{i}", [128, 512], F32) for i in range(2)])
                    ss8r = Ring([sb(st, f"ss8{i}", [128, 8], F32) for i in range(2)])
                trh = ps(st, "trh", [128, 1024], BF16); trhb = Buf()
                trq = Ring([ps(st, f"trq{i}", [128, 1024], BF16) for i in range(2)])
                mmr = Ring([ps(st, f"mm{i}", [128, 512], F32) for i in range(4)])

                ntiles = len(tiles128)
                xt_list = {}

                def issue_load(i):
                    if i >= ntiles:
                        return
                    t0, tt = tiles128[i]
                    xt, xb = xr.next()
                    k.dma(SP, xt[:], xsrc[t0:t0 + 128, :], dXs[i % 3], writes=[xb])
                    xt_list[i] = (xt, xb)

                tstate = {}

                def stage1(i):
                    if i >= ntiles:
                        return
                    issue_load(i + 2)
                    xt, xb = xt_list.pop(i)
                    h, hb = rms_part1(xt[:], xb, g_bc[:], gb, junk, junkb, ssr, rsr, hr)
                    tstate[i] = {"h": h, "hb": hb}

                def stageT(i):
                    if i >= ntiles:
                        return
                    hT, hTb = hTr.next()
                    rms_part2(tstate[i]["h"], tstate[i]["hb"], idb, idbuf, trh, trhb, hT[:], hTb)
                    tstate[i]["hT"] = (hT, hTb)

                cur = {"v": None, "q": None}

                def stageMM(i):
                    t0, tt = tiles128[i]
                    ti = i % 4
                    if ti == 0:
                        cur["v"] = vst.next()
                        cur["q"] = qst.next()
                    vS, vSb = cur["v"]
                    qS, qSb = cur["q"]
                    hT, hTb = tstate[i]["hT"]
                    qk, qkb = qkr.next()
                    nft = (F + 511) // 512
                    chains = []
                    for ft in range(nft):
                        f0 = ft * 512
                        fw = min(512, F - f0)
                        mp, mpb = mmr.next()
                        for c in range(8):
                            k.op(PE, lambda e, c=c, mp=mp, f0=f0, fw=fw, hT=hT: e.matmul(
                                mp[:, 0:fw], hT[:, c, :], wsb[:, c, f0:f0 + fw], start=(c == 0), stop=(c == 7)),
                                reads=[hTb, wb], writes=[mpb], inc=(c == 7))
                        if layer == 0:
                            if ft == 0:
                                k.op(ACT, lambda e, mp=mp, qk=qk: e.activation(out=qk[:, 0:512], in_=mp[:], func=AF.Copy, scale=0.125),
                                     reads=[mpb], writes=[qkb])
                            elif ft == 1:
                                k.op(ACT, lambda e, mp=mp, qk=qk: e.copy(out=qk[:, 512:1024], in_=mp[:]),
                                     reads=[mpb], writes=[qkb])
                            elif ft == 2:
                                k.op(ACT, lambda e, mp=mp, vS=vS, ti=ti: e.copy(out=vS[:, ti, 0:512], in_=mp[:]),
                                     reads=[mpb], writes=[vSb])
                            else:
                                if ft == 3:
                                    nH, gt, gtb, ctab, stab, dcol = 8, gq, gqb, 0, 1, 1024
                                else:
                                    nH, gt, gtb, ctab, stab, dcol = 2, gk, gkb, 2, 3, 1536
                                    k.op(ACT, lambda e, mp=mp, vS=vS, ti=ti: e.copy(out=vS[:, ti, 512:640], in_=mp[:, 128:256]),
                                         reads=[mpb], writes=[vSb])
                                def qkn_chain(mp=mp, mpb=mpb, nH=nH, gt=gt, gtb=gtb, ctab=ctab, stab=stab, dcol=dcol, qk=qk, qkb=qkb, tt=tt):
                                    wdt = nH * 64
                                    sq, sqb = sqr.next()
                                    xn, xnb = xnr.next()
                                    ss8, ss8b = ss8r.next()
                                    tmp, tmpb = tmpr.next()
                                    u, ub = ur.next()
                                    k.op(ACT, lambda e, mp=mp, wdt=wdt, sq=sq: e.activation(out=sq[:, 0:wdt], in_=mp[:, 0:wdt], func=AF.Square, scale=0.125),
                                         reads=[mpb], writes=[sqb])
                                    yield
                                    k.op(DVE, lambda e, wdt=wdt, nH=nH, ss8=ss8, sq=sq: e.tensor_reduce(
                                        out=ss8[:, 0:nH], in_=sq[:, 0:wdt].rearrange("p (h d) -> p h d", h=nH), axis=AX.X, op=ALU.add),
                                        reads=[sqb], writes=[ss8b])
                                    yield
                                    k.op(ACT, lambda e, nH=nH, ss8=ss8: e.activation(out=ss8[:, 0:nH], in_=ss8[:, 0:nH], func=AF.Sqrt,
                                                                            bias=cur_eps["e6"][0][:], scale=1.0),
                                         reads=[ss8b, cur_eps["e6"][1]], writes=[ss8b])
                                    yield
                                    k.op(DVE, lambda e, nH=nH, ss8=ss8: e.reciprocal(out=ss8[:, 0:nH], in_=ss8[:, 0:nH]),
                                         reads=[ss8b], writes=[ss8b])
                                    yield
                                    k.op(DVE, lambda e, mp=mp, wdt=wdt, nH=nH, xn=xn, ss8=ss8: e.tensor_tensor(
                                        out=xn[:, 0:wdt].rearrange("p (h d) -> p h d", h=nH),
                                        in0=mp[:, 0:wdt].rearrange("p (h d) -> p h d", h=nH),
                                        in1=ss8[:, 0:nH].unsqueeze(2).to_broadcast([128, nH, 64]), op=ALU.mult),
                                        reads=[mpb, ss8b], writes=[xnb])
                                    yield
                                    k.op(POOL, lambda e, wdt=wdt, nH=nH, gt=gt, xn=xn: e.tensor_tensor(
                                        out=xn[:, 0:wdt].rearrange("p (h d) -> p h d", h=nH),
                                        in0=xn[:, 0:wdt].rearrange("p (h d) -> p h d", h=nH),
                                        in1=gt[:].unsqueeze(1).to_broadcast([128, nH, 64]), op=ALU.mult),
                                        reads=[xnb, gtb], writes=[xnb])
                                    yield
                                    yield from rope_evac_gen(xn[:, 0:wdt], xnb, wdt, nH, 2, 16, tabs[:, ctab, tt, :], tabs[:, stab, tt, :], tabb,
                                              tmp, tmpb, u, ub, qk[:, dcol:dcol + wdt], qkb)
                                chains.append(qkn_chain())
                        else:
                            if ft < 4:
                                ctab, stab = (0, 1) if ft < 2 else (2, 3)
                                tmp, tmpb = tmpr.next()
                                u, ub = ur.next()
                                rope_evac(mp[:], mpb, 512, 8, 1, 32, tabs[:, ctab, tt, :], tabs[:, stab, tt, :], tabb,
                                          tmp, tmpb, u, ub, qk[:, f0:f0 + 512], qkb)
                            else:
                                k.op(ACT, lambda e, mp=mp, vS=vS, ti=ti, f0=f0: e.copy(out=vS[:, ti, f0 - 2048:f0 - 2048 + 512], in_=mp[:]),
                                     reads=[mpb], writes=[vSb])
                    run_interleaved(chains)
                    tstate[i].update(qk=qk, qkb=qkb, qS=qS, qSb=qSb, vS=vS, vSb=vSb)

                def stageQ(i):
                    if i < 0:
                        return
                    t0, tt = tiles128[i]
                    ti = i % 4
                    stt = tstate.pop(i)
                    qk, qkb, qS, qSb, vS, vSb = stt["qk"], stt["qkb"], stt["qS"], stt["qSb"], stt["vS"], stt["vSb"]
                    c0 = 0
                    while c0 < nchunk:
                        ncb = min(8, nchunk - c0)
                        tq, tqb = trq.next()
                        for c in range(ncb):
                            k.op(PE, lambda e, c=c, c0=c0, tq=tq, qk=qk: e.transpose(
                                out=tq[:, c * 128:(c + 1) * 128], in_=qk[:, (c0 + c) * 128:(c0 + c + 1) * 128], identity=idb[:]),
                                reads=[qkb, idbuf], writes=[tqb], inc=(c == ncb - 1))
                        if layer == 0:
                            k.op(DVE, lambda e, c0=c0, tq=tq, qS=qS, ti=ti: e.tensor_copy(
                                out=qS[:, c0:c0 + 4, :].rearrange("p c (par r q) -> p c par r q", par=2, r=4)[:, :, :, ti, :],
                                in_=tq[:, 0:512].rearrange("p (c par q) -> p c par q", c=4, par=2)),
                                reads=[tqb], writes=[qSb])
                            nk = ncb - 4
                            k.op(DVE, lambda e, c0=c0, nk=nk, tq=tq, qS=qS, ti=ti: e.tensor_copy(
                                out=qS[:, c0 + 4:c0 + 4 + nk, ti * 128:(ti + 1) * 128],
                                in_=tq[:, 512:512 + nk * 128].rearrange("p (c t) -> p c t", c=nk)),
                                reads=[tqb], writes=[qSb])
                        else:
                            k.op(DVE, lambda e, c0=c0, ncb=ncb, tq=tq, qS=qS, ti=ti: e.tensor_copy(
                                out=qS[:, c0:c0 + ncb, ti * 128:(ti + 1) * 128],
                                in_=tq[:, 0:ncb * 128].rearrange("p (c t) -> p c t", c=ncb)),
                                reads=[tqb], writes=[qSb])
                        c0 += ncb
                    if ti == 3:
                        tb0 = t0 - 384
                        sl = (i // 4) % 2
                        k.dma(SP, v_d[tb0:tb0 + 512, :].rearrange("(t p) f -> p t f", p=128), vS[:], dS1[sl],
                              reads=[vSb], store=True)
                        k.dma(SP, qkT_d.rearrange("(c p) t -> p c t", p=128)[:, :, tb0:tb0 + 512], qS[:], dS2[sl],
                              reads=[qSb], store=True)

                issue_load(0)
                issue_load(1)
                stage1(0)
                stageT(0)
                for i in range(ntiles):
                    stage1(i + 1)
                    stageMM(i)
                    stageT(i + 1)
                    stageQ(i - 1)
                stageQ(ntiles - 1)
                k.end_phase()

        def phase_mlp(layer):
            k.begin_phase()
            xsrc = x1 if layer == 0 else x3
            xdst = x2 if layer == 0 else y
            gi = 1 if layer == 0 else 3
            with ExitStack() as st:
                cur_eps["e6"] = mk_col(st, 1e-6)
                dW, dW2, dC = k.dsem(), k.dsem(), k.dsem()
                dS = [k.dsem() for _ in range(1)]
                dXs = [k.dsem() for _ in range(6)]
                wup = sb(st, "wup", [128, 8, 4096], BF16); wub = Buf()
                wdn = sb(st, "wdn", [128, 32, D], BF16); wdb = Buf()
                idb, idbuf, idcast = load_ident(st, dC)
                g_bc, gb = load_bc(st, "g_bc", gains[gi, :], D, dC)
                cb = [gb]
                if layer == 1:
                    gf, gfb = load_bc(st, "gf", gains[4, :], D, dC)
                    cb.append(gfb)
                k.seal(dC, cb)
                idcast()
                nstore = [0]
                xr = Ring([sb(st, f"x{i}", [128, D], F32) for i in range(6)], "x")
                junk = sb(st, "junk", [128, D], BF16); junkb = Buf()
                ssr = Ring([sb(st, f"ss{i}", [128, 1], F32) for i in range(4)])
                rsr = Ring([sb(st, f"rs{i}", [128, 1], F32) for i in range(4)])
                hr = Ring([sb(st, f"h{i}", [128, D], BF16) for i in range(4)])
                hTs = [sb(st, f"hT{i}", [128, 8, 256], BF16) for i in range(2)]
                hTbss = [[Buf(), Buf()], [Buf(), Buf()]]
                aT = sb(st, "aT", [128, 32, 256], BF16); aTb = Buf()
                rr = Ring([sb(st, f"r{i}", [128, 256], F32) for i in range(3)])
                xo = Ring([sb(st, f"xo{i}", [128, D], F32) for i in range(1)])
                trhr = Ring([ps(st, f"trh{i}", [128, 1024], BF16) for i in range(2)])
                upr = Ring([ps(st, f"up{i}", [128, 512], F32) for i in range(2)])
                dnr = Ring([ps(st, f"dn{i}", [128, 512], F32) for i in range(4)])
                ntiles = len(tiles128)
                ngroups = ntiles // 2
                xt_list = {}

                def issue_load(i):
                    if i >= ntiles:
                        return
                    t0, tt = tiles128[i]
                    xt, xb = xr.next()
                    k.dma(SP, xt[:], xsrc[t0:t0 + 128, :], dXs[i % 6], writes=[xb])
                    xt_list[i] = (xt, xb)

                groups = {}

                def norm1(g):
                    if g >= ngroups:
                        return
                    xs = []
                    for j in range(2):
                        i = g * 2 + j
                        xt, xb = xt_list.pop(i)
                        h, hb = rms_part1(xt[:], xb, g_bc[:], gb, junk, junkb, ssr, rsr, hr)
                        xs.append((xt, xb, tiles128[i][0], h, hb))
                    groups[g] = xs

                def norm2(g):
                    if g >= ngroups:
                        return
                    for j in range(2):
                        xt, xb, t0, h, hb = groups[g][j]
                        trh, trhb = trhr.next()
                        rms_part2(h, hb, idb, idbuf, trh, trhb, hTs[g % 2][:, :, j * 128:(j + 1) * 128], hTbss[g % 2][j])

                for i in range(4):
                    issue_load(i)
                load_w_bf16(wup, wub, wupb[layer], 8, dW)
                load_w_bf16(wdn, wdb, wdnb[layer], 32, dW2, step=4)
                norm1(0)
                norm2(0)
                for g in range(ngroups):
                    issue_load(2 * g + 4)
                    issue_load(2 * g + 5)
                    norm1(g + 1)
                    hT = hTs[g % 2]
                    hTbs = hTbss[g % 2]
                    xs = groups.pop(g)
                    for fc in range(32):
                        up, upb = upr.next()
                        for c in range(8):
                            k.op(PE, lambda e, c=c, fc=fc, up=up, hT=hT: e.matmul(
                                up[:, 0:256], wup[:, c, fc * 128:(fc + 1) * 128], hT[:, c, :], start=(c == 0), stop=(c == 7)),
                                reads=[wub, hTbs[0], hTbs[1]], writes=[upb], inc=(c == 7))
                        r, rb = rr.next()
                        k.op(ACT, lambda e, up=up, r=r: e.activation(out=r[:], in_=up[:, 0:256], func=AF.Relu),
                             reads=[upb], writes=[rb])
                        k.op(POOL, lambda e, r=r, fc=fc: e.tensor_tensor(out=aT[:, fc, :], in0=r[:], in1=r[:], op=ALU.mult),
                             reads=[rb], writes=[aTb])
                    norm2(g + 1)
                    for j in range(2):
                        xt, xb, t0 = xs[j][0:3]
                        xot, xob = xo.next()
                        for half in range(2):
                            dn, dnb = dnr.next()
                            for fc in range(32):
                                k.op(PE, lambda e, fc=fc, dn=dn, j=j, half=half: e.matmul(
                                    dn[:], aT[:, fc, j * 128:(j + 1) * 128], wdn[:, fc, half * 512:(half + 1) * 512],
                                    start=(fc == 0), stop=(fc == 31)),
                                    reads=[aTb, wdb], writes=[dnb], inc=(fc == 31))
                            k.op(DVE, lambda e, dn=dn, xt=xt, xot=xot, half=half: e.tensor_tensor(
                                out=xot[:, half * 512:(half + 1) * 512], in0=dn[:], in1=xt[:, half * 512:(half + 1) * 512], op=ALU.add),
                                reads=[dnb, xb], writes=[xob])
                        if layer == 1:
                            ss, ssb = ssr.next()
                            rs, rsb = rsr.next()
                            k.op(ACT, lambda e, xot=xot, ss=ss: e.activation(out=junk[:], in_=xot[:], func=AF.Square, scale=1.0 / 32, accum_out=ss[:]),
                                 reads=[xob], writes=[junkb, ssb])
                            k.op(ACT, lambda e, ss=ss, rs=rs: e.activation(out=rs[:], in_=ss[:], func=AF.Sqrt,
                                                                           bias=cur_eps["e6"][0][:], scale=1.0),
                                 reads=[ssb, cur_eps["e6"][1]], writes=[rsb])
                            k.op(DVE, lambda e, rs=rs: e.reciprocal(out=rs[:], in_=rs[:]), reads=[rsb], writes=[rsb])
                            k.op(DVE, lambda e, xot=xot, rs=rs: e.scalar_tensor_tensor(
                                out=xot[:], in0=xot[:], scalar=rs[:, 0:1], in1=gf[:], op0=ALU.mult, op1=ALU.mult),
                                reads=[xob, rsb, gfb], writes=[xob])
                        k.dma(SP, xdst[t0:t0 + 128, :], xot[:], dS[0], reads=[xob], store=True)
                        nstore[0] += 1
                k.end_phase()

        def run_pending(pend, force=False):
            nxt = []
            for c, f in pend:
                c -= 1
                if c <= 0 or force:
                    r = f()
                    if r is not None:
                        nxt.append(r)
                else:
                    nxt.append((c, f))
            pend[:] = nxt

        def drain(pend):
            while pend:
                run_pending(pend)

        def run_jobs(jobs, LAG=1, pend=None, flush=True):
            n = len(jobs)
            if pend is None:
                pend = []
            for i in range(n + LAG):
                if i < n:
                    jobs[i]["qk"]()
                    jobs[i]["exp"]()
                run_pending(pend)
                if i >= LAG:
                    j = jobs[i - LAG]
                    j["pv"]()
                    if "fin" in j:
                        r = j["fin"]()
                        if r is not None:
                            pend.append(r)
            if flush:
                drain(pend)

        def phase_attn0():
            k.begin_phase()
            with ExitStack() as st:
                dW, dC, dKa, dKb, dVb = k.dsem(), k.dsem(), k.dsem(), k.dsem(), k.dsem()
                dS = [k.dsem() for _ in range(2)]
                dQ = [k.dsem() for _ in range(2)]
                dVe = [k.dsem() for _ in range(2)]
                dVo = [k.dsem() for _ in range(2)]
                dX = [k.dsem() for _ in range(2)]
                Tmax = max(seq_lens)
                wout = sb(st, "wout", [128, 8, D], BF16); wob = Buf()
                dPC = k.dsem()

                def late_loads():
                    load_w_cast(st, wout, wob, w_out_e, 8, dW, after=osbbs[0])
                    precast([(w_up[0], wupb[0]), (w_down[0], wdnb[0]), (w_in_o, winob), (w_out_o, woutob)], dPC)
                idb, idbuf, idcast = load_ident(st, dC)
                ones = sb(st, "ones", [128, 64], BF16); onesb = Buf()
                k.op(DVE, lambda e: e.memset(ones[:], 1.0), writes=[onesb])
                PT = sb(st, "PT", [128, 8, 14 * 64], BF16); ptb = Buf()
                mk = sb(st, "mk", [128, 14 * 64], F32); mkb = Buf()
                k.dma(SP, mk[:], naMask.rearrange("p a j q -> p (a j q)"), dC, writes=[mkb])
                k.seal(dC, [mkb])
                idcast()
                gst = Ring([sb(st, f"gst{i}", [128, 14 * 64], F32) for i in range(2)])
                dG = [k.dsem() for _ in range(2)]
                for h in range(8):
                    g_, gb_ = gst.next()
                    k.dma(SP, g_[:], rpbG[:, h].rearrange("p a j q -> p (a j q)"), dG[h % 2], writes=[gb_])
                    k.op(DVE, lambda e, g_=g_, h=h: e.tensor_tensor(out=PT[:, h, :], in0=g_[:], in1=mk[:], op=ALU.add),
                         reads=[gb_, mkb], writes=[ptb])
                KTa = sb(st, "KTa", [128, 4, Tmax], BF16); ktab = Buf()
                KTb = sb(st, "KTb", [128, 2, Tmax], BF16); ktbb = Buf()
                Vb = sb(st, "Vb", [128, Tmax // 128, 128], BF16); vbb = Buf()
                Vwe = Ring([sb(st, f"Vwe{i}", [128, 8, 512], BF16) for i in range(2)])
                Vwo = Ring([sb(st, f"Vwo{i}", [128, 8, 512], BF16) for i in range(2)])
                QTr = Ring([sb(st, f"QT{i}", [128, 8, 512], BF16) for i in range(2)])
                Pr = Ring([sb(st, f"P{i}", [128, 2, 512], BF16) for i in range(3)])
                osbs = [sb(st, f"osb{i}", [128, 8, 512], BF16) for i in range(2)]
                osbbs = [Buf(), Buf()]
                rbcr = Ring([sb(st, f"rbc{i}", [128, 512], F32) for i in range(2)])
                xr = Ring([sb(st, f"x{i}", [128, D], F32) for i in range(2)])
                xo = Ring([sb(st, f"xo{i}", [128, D], F32) for i in range(2)])
                Sr = Ring([ps(st, f"S{i}", [128, 2, 512], F32) for i in range(2)])
                Or = Ring([ps(st, f"O{i}", [128, 512], F32) for i in range(2)])
                Ur = Ring([ps(st, f"U{i}", [128, 512], F32) for i in range(2)])
                qkv = qk0T.rearrange("(c p) t -> p c t", p=128)
                xcount = 0
                blocks = [(s, qb) for s, T in enumerate(seq_lens) for qb in range(T // 512)]
                loaded = {}

                def issue_block(bi):
                    if bi >= len(blocks):
                        return
                    s, qb = blocks[bi]
                    T = seq_lens[s]
                    base = seq_base[s]
                    R = T // 64
                    q0 = base + qb * 512
                    QT, QTb = QTr.next()
                    k.dma(SP, QT[:, 0:4, :], qkv[:, 0:4, q0:q0 + 512], dQ[bi % 2], writes=[QTb])
                    k.dma(SP, QT[:, 4:8, :], qkv[:, 8:12, q0:q0 + 512], dQ[bi % 2], writes=[QTb])
                    units = _na_units(qb * 8, R)
                    ks_e = sorted({u[0] for u in units if u[0] % 2 == 0})
                    ks_o = sorted({u[0] for u in units if u[0] % 2 == 1})
                    Ve, Veb = Vwe.next()
                    Vo, Vob = Vwo.next()
                    ne = (ks_e[-1] - ks_e[0]) // 2 + 1
                    no = (ks_o[-1] - ks_o[0]) // 2 + 1
                    assert ne <= 8 and no <= 8
                    te = base + ks_e[0] * 64
                    k.dma(SP, Ve[:, 0:ne, :], v0[te:te + ne * 128, 0:512].rearrange("(t p) f -> p t f", p=128),
                          dVe[bi % 2], writes=[Veb])
                    to = base + ks_o[0] * 64
                    k.dma(SP, Vo[:, 0:no, :], v0[to:to + no * 128, 0:512].rearrange("(t p) f -> p t f", p=128),
                          dVo[bi % 2], writes=[Vob])
                    loaded[bi] = (QT, QTb, units, ks_e, ks_o, Ve, Veb, Vo, Vob)

                pend = []
                prev_out = None
                xc = {"n": 0}

                def make_outproj(tt, q0, osb, osbb):
                    def f():
                        t0 = q0 + tt * 128
                        xt, xb = xr.next()
                        k.dma(SP, xt[:], xin[t0:t0 + 128, :], dX[xc["n"] % 2], writes=[xb])
                        xc["n"] += 1
                        xot, xob = xo.next()
                        S2, Sb_ = Sr.next()
                        for half in range(2):
                            for pr in range(8):
                                k.op(PE, lambda e, pr=pr, half=half: e.matmul(
                                    S2[:, half, :], osb[:, pr, tt * 128:(tt + 1) * 128], wout[:, pr, half * 512:(half + 1) * 512],
                                    start=(pr == 0), stop=(pr == 7)), reads=[osbb, wob], writes=[Sb_], inc=(pr == 7 and half == 1))
                        k.op(DVE, lambda e: e.tensor_tensor(
                            out=xot[:].rearrange("p (h f) -> p h f", h=2), in0=S2[:], in1=xt[:].rearrange("p (h f) -> p h f", h=2), op=ALU.add),
                            reads=[Sb_, xb], writes=[xob])
                        k.dma(SP, x1[t0:t0 + 128, :], xot[:], dS[xc["n"] % 2], reads=[xob], store=True)
                        return None
                    return f

                issue_block(0)
                for bi, (s, qb) in enumerate(blocks):
                    T = seq_lens[s]
                    base = seq_base[s]
                    if qb == 0:
                        k.dma(SP, KTa[:, :, 0:T], qkv[:, 4:8, base:base + T], dKa, writes=[ktab])
                        k.dma(SP, KTb[0:64, 0, 0:T], qk0T[12 * 128:12 * 128 + 64, base:base + T], dKb, writes=[ktbb])
                        k.dma(SP, KTb[64:128, 0, 0:T], qk0T[12 * 128:12 * 128 + 64, base:base + T], dKb, writes=[ktbb])
                        k.dma(SP, KTb[0:64, 1, 0:T], qk0T[12 * 128 + 64:13 * 128, base:base + T], dKb, writes=[ktbb])
                        k.dma(SP, KTb[64:128, 1, 0:T], qk0T[12 * 128 + 64:13 * 128, base:base + T], dKb, writes=[ktbb])
                        n4 = T // 512
                        for c4 in range(4):
                            k.dma(SP, Vb[:, c4 * n4:(c4 + 1) * n4, :],
                                  v0[base + c4 * n4 * 128:base + (c4 + 1) * n4 * 128, 512:640].rearrange("(t p) f -> p t f", p=128),
                                  dVb, writes=[vbb])
                    issue_block(bi + 1)
                    q0 = base + qb * 512
                    QT, QTb, units, ks_e, ks_o, Ve, Veb, Vo, Vob = loaded.pop(bi)
                    osb, osbb = osbs[bi % 2], osbbs[bi % 2]
                    jobs = []
                    for pr in range(8):
                        Oacc, Ob = Or.next()
                        Uacc, Ub = Ur.next()
                        first = {"v": True}
                        if pr < 4:
                            pc = pr
                            ha, hb = 2 * pr, 2 * pr + 1
                            for (ks, pos0, n, jpar, j20) in units:
                                nw = n * 64
                                c0 = pos0 * 64
                                jc0 = (jpar * 7 + j20) * 64
                                if ks % 2 == 0:
                                    Vt, Vtb, vi = Ve, Veb, (ks - ks_e[0]) // 2
                                else:
                                    Vt, Vtb, vi = Vo, Vob, (ks - ks_o[0]) // 2
                                cell = {}

                                def qk(cell=cell, pc=pc, ks=ks, c0=c0, nw=nw, ha=ha, hb=hb, jc0=jc0, QT=QT, QTb=QTb):
                                    S2, Sb_ = Sr.next()
                                    cell["S"] = (S2, Sb_)
                                    k.op(PE, lambda e: e.matmul(S2[:, 0, 0:nw], KTa[0:64, pc, ks * 64:ks * 64 + 128],
                                                                QT[0:64, pc, c0:c0 + nw], start=True, stop=False),
                                         reads=[ktab, QTb], writes=[Sb_], inc=False)
                                    k.op(PE, lambda e: e.matmul(S2[:, 1, 0:nw], KTa[64:128, pc, ks * 64:ks * 64 + 128],
                                                                QT[64:128, pc, c0:c0 + nw], start=True, stop=False),
                                         reads=[ktab, QTb], writes=[Sb_], inc=False)
                                    k.op(PE, lambda e: e.matmul(S2[:, 0, 0:nw], idb[:], PT[:, ha, jc0:jc0 + nw], start=False, stop=True),
                                         reads=[idbuf, ptb], writes=[Sb_], inc=False)
                                    k.op(PE, lambda e: e.matmul(S2[:, 1, 0:nw], idb[:], PT[:, hb, jc0:jc0 + nw], start=False, stop=True),
                                         reads=[idbuf, ptb], writes=[Sb_])

                                def ex(cell=cell, nw=nw):
                                    S2, Sb_ = cell["S"]
                                    P2, Pb_ = Pr.next()
                                    cell["P"] = (P2, Pb_)
                                    k.op(ACT, lambda e: e.activation(out=P2[:, :, 0:nw], in_=S2[:, :, 0:nw], func=AF.Exp),
                                         reads=[Sb_], writes=[Pb_])

                                def pv(cell=cell, nw=nw, c0=c0, Oacc=Oacc, Ob=Ob, Uacc=Uacc, Ub=Ub, Vt=Vt, Vtb=Vtb, vi=vi,
                                       ha=ha, hb=hb, first=first):
                                    P2, Pb_ = cell["P"]
                                    st_ = first["v"]
                                    first["v"] = False
                                    k.op(PE, lambda e: e.matmul(Oacc[0:64, c0:c0 + nw], Vt[:, vi, ha * 64:(ha + 1) * 64], P2[:, 0, 0:nw],
                                                                start=st_, stop=False, skip_group_check=True),
                                         reads=[Pb_, Vtb], writes=[Ob], inc=False)
                                    k.op(PE, lambda e: e.matmul(Oacc[64:128, c0:c0 + nw], Vt[:, vi, hb * 64:(hb + 1) * 64], P2[:, 1, 0:nw],
                                                                start=st_, stop=False, skip_group_check=True, tile_position=(0, 64)),
                                         reads=[Pb_, Vtb], writes=[Ob], inc=False)
                                    k.op(PE, lambda e: e.matmul(Uacc[0:64, c0:c0 + nw], ones[:, 0:64], P2[:, 0, 0:nw],
                                                                start=st_, stop=False, skip_group_check=True),
                                         reads=[Pb_, onesb], writes=[Ub], inc=False)
                                    k.op(PE, lambda e: e.matmul(Uacc[64:128, c0:c0 + nw], ones[:, 0:64], P2[:, 1, 0:nw],
                                                                start=st_, stop=False, skip_group_check=True, tile_position=(0, 64)),
                                         reads=[Pb_, onesb], writes=[Ub])
                                jobs.append({"qk": qk, "exp": ex, "pv": pv})
                        else:
                            gi = pr - 4
                            kv = gi // 2
                            pc = 4 + gi
                            nkt = T // 128
                            for kt in range(nkt):
                                cell = {}

                                def qk(cell=cell, pc=pc, kv=kv, kt=kt, QT=QT, QTb=QTb):
                                    S2, Sb_ = Sr.next()
                                    cell["S"] = (S2, Sb_)
                                    k.op(PE, lambda e: e.matmul(S2[:, 0, :], KTb[0:64, kv, kt * 128:(kt + 1) * 128], QT[0:64, pc, :],
                                                                start=True, stop=True), reads=[ktbb, QTb], writes=[Sb_], inc=False)
                                    k.op(PE, lambda e: e.matmul(S2[:, 1, :], KTb[64:128, kv, kt * 128:(kt + 1) * 128], QT[64:128, pc, :],
                                                                start=True, stop=True), reads=[ktbb, QTb], writes=[Sb_])

                                def ex(cell=cell):
                                    S2, Sb_ = cell["S"]
                                    P2, Pb_ = Pr.next()
                                    cell["P"] = (P2, Pb_)
                                    k.op(ACT, lambda e: e.activation(out=P2[:], in_=S2[:], func=AF.Exp), reads=[Sb_], writes=[Pb_])

                                def pv(cell=cell, Oacc=Oacc, Ob=Ob, Uacc=Uacc, Ub=Ub, kt=kt, kv=kv, nkt=nkt):
                                    P2, Pb_ = cell["P"]
                                    a, z = (kt == 0), (kt == nkt - 1)
                                    k.op(PE, lambda e: e.matmul(Oacc[0:64, :], Vb[:, kt, kv * 64:(kv + 1) * 64], P2[:, 0, :], start=a, stop=z),
                                         reads=[Pb_, vbb], writes=[Ob], inc=False)
                                    k.op(PE, lambda e: e.matmul(Oacc[64:128, :], Vb[:, kt, kv * 64:(kv + 1) * 64], P2[:, 1, :], start=a, stop=z,
                                                                tile_position=(0, 64)), reads=[Pb_, vbb], writes=[Ob], inc=False)
                                    k.op(PE, lambda e: e.matmul(Uacc[0:64, :], ones[:, 0:64], P2[:, 0, :], start=a, stop=z),
                                         reads=[Pb_, onesb], writes=[Ub], inc=False)
                                    k.op(PE, lambda e: e.matmul(Uacc[64:128, :], ones[:, 0:64], P2[:, 1, :], start=a, stop=z,
                                                                tile_position=(0, 64)), reads=[Pb_, onesb], writes=[Ub])
                                jobs.append({"qk": qk, "exp": ex, "pv": pv})

                        def fin(Oacc=Oacc, Ob=Ob, Uacc=Uacc, Ub=Ub, pr=pr, osb=osb, osbb=osbb):
                            rbc, rbcb = rbcr.next()
                            k.op(DVE, lambda e: e.reciprocal(out=rbc[:], in_=Uacc[:]), reads=[Ub], writes=[rbcb])
                            k.op(DVE, lambda e: e.tensor_tensor(
                                out=osb[:, pr, :].rearrange("p (r par c) -> p par r c", r=4, par=2),
                                in0=Oacc[:].rearrange("p (par r c) -> p par r c", par=2, r=4),
                                in1=rbc[:].rearrange("p (par r c) -> p par r c", par=2, r=4), op=ALU.mult),
                                reads=[Ob, rbcb], writes=[osbb])
                        jobs[-1]["fin"] = fin
                    if prev_out is not None:
                        for tt in range(4):
                            pend.append((6 + 2 * tt, make_outproj(tt, *prev_out)))
                    run_jobs(jobs, pend=pend, flush=False)
                    if bi == 0:
                        late_loads()
                    prev_out = (q0, osb, osbb)
                for tt in range(4):
                    pend.append((2 + 2 * tt, make_outproj(tt, *prev_out)))
                drain(pend)
                k.end_phase()

        def phase_attn1():
            k.begin_phase()
            with ExitStack() as st:
                dW, dC, dK, dV = k.dsem(), k.dsem(), k.dsem(), k.dsem()
                dS = [k.dsem() for _ in range(2)]
                dQ = [k.dsem() for _ in range(2)]
                dX = [k.dsem() for _ in range(2)]
                Tmax = max(seq_lens)
                wout = sb(st, "wout", [128, 8, D], BF16); wob = Buf()
                dPC = k.dsem()

                def late_loads():
                    load_w_bf16(wout, wob, woutob, 8, dW, step=4)
                    precast([(w_up[1], wupb[1]), (w_down[1], wdnb[1])], dPC)
                ones = sb(st, "ones", [128, 128], BF16); onesb = Buf()
                onesS = sb(st, "onesS", [128, 128], BF16)
                e5, e5b = mk_col(st, 1e-5)
                k.op(DVE, lambda e: e.memset(ones[:], 1.0), writes=[onesb])
                k.op(DVE, lambda e: e.memset(onesS[:], 1.0 / 128), writes=[onesb])
                lv = sb(st, "lv", [128, 4, 64], F32); lvb = Buf()
                for i in range(4):
                    k.dma(SP, lv[:, i, :], lamv[i, :].partition_broadcast(128), dC, writes=[lvb])
                gs = sb(st, "gs", [128, 1], F32); gsb = Buf()
                k.dma(SP, gs[:], subg[:, :], dC, writes=[gsb])
                k.seal(dC, [lvb, gsb])
                lp = sb(st, "lp", [128, 2, 64], F32); lpb = Buf()
                l2 = sb(st, "l2", [128, 2], F32); l2b = Buf()
                nlam = sb(st, "nlam", [128, 1], F32); nlamb = Buf()
                k.op(DVE, lambda e: e.tensor_tensor(out=lp[:, 0, :], in0=lv[:, 0, :], in1=lv[:, 1, :], op=ALU.mult), reads=[lvb], writes=[lpb])
                k.op(DVE, lambda e: e.tensor_tensor(out=lp[:, 1, :], in0=lv[:, 2, :], in1=lv[:, 3, :], op=ALU.mult), reads=[lvb], writes=[lpb])
                k.op(DVE, lambda e: e.tensor_reduce(out=l2[:], in_=lp[:], axis=AX.X, op=ALU.add), reads=[lpb], writes=[l2b])
                k.op(ACT, lambda e: e.activation(out=l2[:], in_=l2[:], func=AF.Exp), reads=[l2b], writes=[l2b])
                k.op(DVE, lambda e: e.tensor_tensor(out=nlam[:], in0=l2[:, 1:2], in1=l2[:, 0:1], op=ALU.subtract), reads=[l2b], writes=[nlamb])
                k.op(DVE, lambda e: e.tensor_scalar(out=nlam[:], in0=nlam[:], scalar1=-LAM_INIT1, scalar2=1.0, op0=ALU.add, op1=ALU.mult),
                     reads=[nlamb], writes=[nlamb])
                k.op(DVE, lambda e: e.tensor_scalar(out=gs[:], in0=gs[:], scalar1=(1.0 - LAM_INIT1), scalar2=0.0, op0=ALU.mult, op1=ALU.add),
                     reads=[gsb], writes=[gsb])
                KT = sb(st, "KT", [128, 8, Tmax], BF16); ktb = Buf()
                V = sb(st, "V", [128, Tmax // 128, D], BF16); vb = Buf()
                QTr = Ring([sb(st, f"QT{i}", [128, 8, 512], BF16) for i in range(2)])
                Pr = Ring([sb(st, f"P{i}", [128, 2, 512], BF16) for i in range(3)])
                osbs = [sb(st, f"osb{i}", [128, 8, 512], BF16) for i in range(2)]
                osbbs = [Buf(), Buf()]
                o1 = sb(st, "o1", [128, 512], F32); o1b = Buf()
                o2 = sb(st, "o2", [128, 512], F32); o2b = Buf()
                us = sb(st, "us", [128, 512], F32); usb = Buf()
                sel0 = sb(st, "sel0", [128, 128], F32)
                sel1 = sb(st, "sel1", [128, 128], F32)
                selb = Buf()
                k.op(DVE, lambda e: e.memset(sel0[0:64, :], 1.0 / 64), writes=[selb])
                k.op(DVE, lambda e: e.memset(sel0[64:128, :], 0.0), writes=[selb])
                k.op(DVE, lambda e: e.memset(sel1[0:64, :], 0.0), writes=[selb])
                k.op(DVE, lambda e: e.memset(sel1[64:128, :], 1.0 / 64), writes=[selb])
                sqt = sb(st, "sqt", [128, 512], BF16); sqb = Buf()
                rst = sb(st, "rst", [128, 512], F32); rstb = Buf()
                xr = Ring([sb(st, f"x{i}", [128, D], F32) for i in range(1)])
                xo = Ring([sb(st, f"xo{i}", [128, D], F32) for i in range(2)])
                Sr = Ring([ps(st, f"S{i}", [128, 2, 512], F32) for i in range(2)])
                O1 = ps(st, "O1", [128, 512], F32); O1b = Buf()
                O2 = ps(st, "O2", [128, 512], F32); O2b = Buf()
                U = ps(st, "U", [128, 512], F32); Ub = Buf()
                qkv = qk1T.rearrange("(c p) t -> p c t", p=128)
                xcount = 0
                blocks = [(s, qb) for s, T in enumerate(seq_lens) for qb in range(T // 512)]
                loaded = {}

                def issue_block(bi):
                    if bi >= len(blocks):
                        return
                    s, qb = blocks[bi]
                    q0 = seq_base[s] + qb * 512
                    QT, QTb = QTr.next()
                    k.dma(SP, QT[:], qkv[:, 0:8, q0:q0 + 512], dQ[bi % 2], writes=[QTb])
                    loaded[bi] = (QT, QTb)

                pend = []
                prev_out = None
                xc = {"n": 0}

                def make_outproj(tt, q0, osb, osbb):
                    def f():
                        t0 = q0 + tt * 128
                        xt, xb = xr.next()
                        k.dma(SP, xt[:], x2[t0:t0 + 128, :], dX[xc["n"] % 2], writes=[xb])
                        xc["n"] += 1
                        xot, xob = xo.next()
                        S2, Sb_ = Sr.next()
                        for half in range(2):
                            for h in range(8):
                                k.op(PE, lambda e, h=h, half=half: e.matmul(
                                    S2[:, half, :], osb[:, h, tt * 128:(tt + 1) * 128], wout[:, h, half * 512:(half + 1) * 512],
                                    start=(h == 0), stop=(h == 7)), reads=[osbb, wob], writes=[Sb_], inc=(h == 7 and half == 1))
                        k.op(DVE, lambda e: e.tensor_tensor(
                            out=xot[:].rearrange("p (h f) -> p h f", h=2), in0=S2[:], in1=xt[:].rearrange("p (h f) -> p h f", h=2), op=ALU.add),
                            reads=[Sb_, xb], writes=[xob])
                        k.dma(SP, x3[t0:t0 + 128, :], xot[:], dS[xc["n"] % 2], reads=[xob], store=True)
                        return None
                    return f

                issue_block(0)
                for bi, (s, qb) in enumerate(blocks):
                    T = seq_lens[s]
                    base = seq_base[s]
                    nkt = T // 128
                    if qb == 0:
                        for c in range(8):
                            k.dma(SP, KT[:, c, 0:T], qkv[:, 8 + c, base:base + T], dK, writes=[ktb])
                        for c in range(4):
                            n4 = nkt // 4
                            k.dma(SP, V[:, c * n4:(c + 1) * n4, :],
                                  v1[base + c * n4 * 128:base + (c + 1) * n4 * 128, :].rearrange("(t p) f -> p t f", p=128), dV, writes=[vb])
                    issue_block(bi + 1)
                    q0 = base + qb * 512
                    QT, QTb = loaded.pop(bi)
                    osb, osbb = osbs[bi % 2], osbbs[bi % 2]
                    jobs = []
                    for h in range(8):
                        for kt in range(nkt):
                            cell = {}

                            def qk(cell=cell, h=h, kt=kt, QT=QT, QTb=QTb):
                                S2, Sb_ = Sr.next()
                                cell["S"] = (S2, Sb_)
                                k.op(PE, lambda e: e.matmul(S2[:, 0, :], KT[0:64, h, kt * 128:(kt + 1) * 128], QT[0:64, h, :],
                                                            start=True, stop=True), reads=[ktb, QTb], writes=[Sb_], inc=False)
                                k.op(PE, lambda e: e.matmul(S2[:, 1, :], KT[64:128, h, kt * 128:(kt + 1) * 128], QT[64:128, h, :],
                                                            start=True, stop=True), reads=[ktb, QTb], writes=[Sb_])

                            def ex(cell=cell):
                                S2, Sb_ = cell["S"]
                                P2, Pb_ = Pr.next()
                                cell["P"] = (P2, Pb_)
                                k.op(ACT, lambda e: e.activation(out=P2[:], in_=S2[:], func=AF.Exp), reads=[Sb_], writes=[Pb_])

                            def pv(cell=cell, kt=kt, h=h, nkt=nkt):
                                P2, Pb_ = cell["P"]
                                a, z = (kt == 0), (kt == nkt - 1)
                                k.op(PE, lambda e: e.matmul(O1[:], V[:, kt, h * 128:(h + 1) * 128], P2[:, 0, :], start=a, stop=z),
                                     reads=[Pb_, vb], writes=[O1b], inc=False)
                                k.op(PE, lambda e: e.matmul(O2[:], V[:, kt, h * 128:(h + 1) * 128], P2[:, 1, :], start=a, stop=z),
                                     reads=[Pb_, vb], writes=[O2b], inc=False)
                                k.op(PE, lambda e: e.matmul(U[0:64, :], ones[:, 0:64], P2[:, 0, :], start=a, stop=z),
                                     reads=[Pb_, onesb], writes=[Ub], inc=False)
                                k.op(PE, lambda e: e.matmul(U[64:128, :], ones[:, 0:64], P2[:, 1, :], start=a, stop=z,
                                                            tile_position=(0, 64)),
                                     reads=[Pb_, onesb], writes=[Ub])
                            jobs.append({"qk": qk, "exp": ex, "pv": pv})

                        def fin(h=h, osb=osb, osbb=osbb):
                            k.op(DVE, lambda e: e.tensor_copy(out=o1[:], in_=O1[:]), reads=[O1b], writes=[o1b])
                            k.op(DVE, lambda e: e.tensor_copy(out=o2[:], in_=O2[:]), reads=[O2b], writes=[o2b])
                            k.op(DVE, lambda e: e.tensor_copy(out=us[:], in_=U[:]), reads=[Ub], writes=[usb])

                            def fin2(h=h):
                                k.op(DVE, lambda e: e.reciprocal(out=us[:], in_=us[:]), reads=[usb], writes=[usb])

                                def fin2b(h=h):
                                    S2, Sb_ = Sr.next()
                                    k.op(PE, lambda e: e.matmul(S2[:, 0, :], sel0[:], us[:], start=True, stop=True),
                                         reads=[usb, selb], writes=[Sb_], inc=False)
                                    k.op(PE, lambda e: e.matmul(S2[:, 1, :], sel1[:], us[:], start=True, stop=True),
                                         reads=[usb, selb], writes=[Sb_])
                                    k.op(DVE, lambda e: e.tensor_tensor(out=o1[:], in0=o1[:], in1=S2[:, 0, :], op=ALU.mult),
                                         reads=[o1b, Sb_], writes=[o1b])
                                    k.op(DVE, lambda e: e.tensor_tensor(out=o2[:], in0=o2[:], in1=S2[:, 1, :], op=ALU.mult),
                                         reads=[o2b, Sb_], writes=[o2b])
                                    k.op(DVE, lambda e: e.scalar_tensor_tensor(out=o1[:], in0=o2[:], scalar=nlam[:, 0:1], in1=o1[:],
                                                                               op0=ALU.mult, op1=ALU.add),
                                         reads=[o2b, nlamb, o1b], writes=[o1b])
                                    k.op(POOL, lambda e: e.tensor_tensor(out=sqt[:], in0=o1[:], in1=o1[:], op=ALU.mult),
                                         reads=[o1b], writes=[sqb])
                                    return (4, fin3)

                                def fin3(h=h):
                                    S2, Sb_ = Sr.next()
                                    k.op(PE, lambda e: e.matmul(S2[:, 0, :], onesS[:], sqt[:], start=True, stop=True),
                                         reads=[sqb, onesb], writes=[Sb_])
                                    k.op(ACT, lambda e: e.activation(out=rst[:], in_=S2[:, 0, :], func=AF.Ln, bias=e5[:], scale=1.0),
                                         reads=[Sb_, e5b], writes=[rstb])
                                    k.op(ACT, lambda e: e.activation(out=rst[:], in_=rst[:], func=AF.Exp, scale=-0.5),
                                         reads=[rstb], writes=[rstb])
                                    k.op(DVE, lambda e: e.scalar_tensor_tensor(out=osb[:, h, :], in0=o1[:], scalar=gs[:, 0:1], in1=rst[:],
                                                                               op0=ALU.mult, op1=ALU.mult),
                                         reads=[o1b, gsb, rstb], writes=[osbb])
                                return (4, fin2b)
                            return (1, fin2)
                        jobs[-1]["fin"] = fin
                    if prev_out is not None:
                        for tt in range(4):
                            pend.append((12 + 2 * tt, make_outproj(tt, *prev_out)))
                    run_jobs(jobs, pend=pend, flush=False)
                    if bi == 0:
                        late_loads()
                    prev_out = (q0, osb, osbb)
                for tt in range(4):
                    pend.append((12 + 2 * tt, make_outproj(tt, *prev_out)))
                drain(pend)
                k.end_phase()

        phases = [lambda: phase_inproj(0), phase_attn0, lambda: phase_mlp(0),
                  lambda: phase_inproj(1), phase_attn1, lambda: phase_mlp(1)]
        for p in phases[:nphases]:
            p()
    return nc


_ROPE = None


def _consts(rpb):
    global _ROPE
    if _ROPE is None:
        _ROPE = _rope_tables()
    g, m = _na_consts(np.asarray(rpb, dtype=np.float32)[0])
    return {"ropeT": _ROPE, "rpbG": g, "naMask": m, "identF": np.eye(128, dtype=np.float32)}


def make_in_maps(x_list, p):
    f = lambda a: np.ascontiguousarray(np.asarray(a, dtype=np.float32))
    c = _consts(p["rpb"])
    shared = {
        "w_in_e": f(p["w_in_e"][0]), "w_out_e": f(p["w_out_e"][0]),
        "w_in_o": f(p["w_in_o"][0]), "w_out_o": f(p["w_out_o"][0]),
        "w_up": f(p["w_up"]), "w_down": f(p["w_down"]),
        "gains": f(np.stack([p["ln_mix_e"][0], p["ln_mlp"][0], p["ln_mix_o"][0], p["ln_mlp"][1], p["ln_f"]])),
        "qkn": f(np.stack([p["q_norm_b"][0], p["k_norm_b"][0]])),
        "lamv": f(np.stack([p["lambda_q1"][0], p["lambda_k1"][0], p["lambda_q2"][0], p["lambda_k2"][0]])),
        "subg": f(np.asarray(p["subln_g"][0]).reshape(128, 1)),
        **c,
    }
    return [dict(shared, xin=f(x)) for x in x_list]


def kernel(x_prompt, x_sample, **p):
    x_prompt = np.asarray(x_prompt, dtype=np.float32)
    x_sample = np.asarray(x_sample, dtype=np.float32)
    n = 8
    seq_lens = [2048, 4096, 4096]
    xs = []
    for c in range(n):
        xs.append(np.concatenate([x_prompt[c], x_sample[2 * c], x_sample[2 * c + 1]], axis=0))
    nc = build(seq_lens)
    in_maps = make_in_maps(xs, p)
    res = run_bass_kernel_spmd(nc, in_maps, core_ids=list(range(n)))
    yp = np.empty((8, 2048, D), np.float32)
    ysm = np.empty((16, 4096, D), np.float32)
    for c in range(n):
        yy = res.results[c]["y"]
        yp[c] = yy[0:2048]
        ysm[2 * c] = yy[2048:6144]
        ysm[2 * c + 1] = yy[6144:10240]
    return (yp, ysm)
```
